# Optimizing a Trainium2 kernel written in Bass

```python
import math
import jax, jax.numpy as jnp
from jax import lax
import numpy as np

D_MODEL = 1024
BATCH = 16
SEQ = 4096
DEPTH = 1
DEC_BATCH = 8
DEC_SEQ = 8192
PAST_LEN = 128

HEAD_DIM = 64
N_HEADS_A = 8
N_HEADS_B = 8
WIDTH_A = N_HEADS_A * HEAD_DIM
WIDTH_B = N_HEADS_B * HEAD_DIM
MIX_WIDTH = WIDTH_A + WIDTH_B
DILATED_PATTERNS = ((128, 1), (512, 4), (2048, 16))
GRID_W = 64
NA_ROWS_MAX = 8
NA_COLS = 16
N_MEM = 256
N_HEADS_X = 4
HEAD_DIM_X = D_MODEL // N_HEADS_X
D_FF = 4 * D_MODEL
ROPE_THETA = 10000.0
LN_EPS = 1e-5
ALPHA = (2 * DEPTH) ** 0.25
BETA = (8 * DEPTH) ** -0.25
NEG_INF = -1e30

kernel_name = 'hybrid_dilated_neighbourhood_encoder'


def layer_norm(x, g, b):
    xf = x.astype(jnp.float32)
    mu = jnp.mean(xf, axis=-1, keepdims=True)
    var = jnp.mean(jnp.square(xf - mu), axis=-1, keepdims=True)
    y = (xf - mu) * lax.rsqrt(var + LN_EPS)
    return (y * g.astype(jnp.float32) + b.astype(jnp.float32)).astype(x.dtype)


def rms_norm(x, g):
    xf = x.astype(jnp.float32)
    y = xf * lax.rsqrt(jnp.mean(jnp.square(xf), axis=-1, keepdims=True) + LN_EPS)
    return y * g.astype(jnp.float32)


def rotary(x):
    T = x.shape[1]
    half = HEAD_DIM // 2
    inv = ROPE_THETA ** (-jnp.arange(half, dtype=jnp.float32) / half)
    ang = jnp.arange(T, dtype=jnp.float32)[:, None] * inv[None, :]
    cos = jnp.cos(ang)[None, :, None, :]
    sin = jnp.sin(ang)[None, :, None, :]
    x1, x2 = x[..., :half], x[..., half:]
    return jnp.concatenate([x1 * cos - x2 * sin, x2 * cos + x1 * sin], axis=-1)


def dilated_window_attention(q, k, v, window, dilation):
    B, T, H, hd = q.shape
    n_side = (window // 2) // dilation
    L = T // dilation
    nb = -(-L // n_side)
    Lp = nb * n_side

    def to_sub(a):
        return a.reshape(B, L, dilation, H, hd).transpose(0, 2, 1, 3, 4)

    qs, ks, vs = to_sub(q), to_sub(k), to_sub(v)
    qs = jnp.pad(qs, ((0, 0), (0, 0), (0, Lp - L), (0, 0), (0, 0)))
    kpad = ((0, 0), (0, 0), (n_side, Lp - L + n_side), (0, 0), (0, 0))
    ks = jnp.pad(ks, kpad)
    vs = jnp.pad(vs, kpad)
    qb = qs.reshape(B, dilation, nb, n_side, H, hd)
    kidx = n_side * jnp.arange(nb)[:, None] + jnp.arange(3 * n_side)[None, :]
    kb = ks[:, :, kidx]
    vb = vs[:, :, kidx]
    qi = n_side * jnp.arange(nb)[:, None] + jnp.arange(n_side)[None, :]
    kj = kidx - n_side
    valid = ((kj[:, None, :] >= 0) & (kj[:, None, :] < L)
             & (jnp.abs(kj[:, None, :] - qi[:, :, None]) <= n_side))
    s = jnp.einsum('bdnqhc,bdnkhc->bdnhqk', qb, kb) * (hd ** -0.5)
    s = jnp.where(valid[None, None, :, None], s, NEG_INF)
    m = jnp.max(s, axis=-1, keepdims=True)
    p = jnp.exp(s - m)
    den = jnp.sum(p, axis=-1)
    o = jnp.einsum('bdnhqk,bdnkhc->bdnqhc', p, vb) / den.transpose(0, 1, 2, 4, 3)[..., None]
    lse = m[..., 0] + jnp.log(den)
    o = o.reshape(B, dilation, Lp, H, hd)[:, :, :L].transpose(0, 2, 1, 3, 4).reshape(B, T, H, hd)
    lse = lse.transpose(0, 1, 2, 4, 3).reshape(B, dilation, Lp, H)[:, :, :L]
    lse = lse.transpose(0, 2, 1, 3).reshape(B, T, H)
    return o, lse


def dilated_mixture_attention(q, k, v):
    outs, lses = [], []
    for window, dilation in DILATED_PATTERNS:
        o, lse = dilated_window_attention(q, k, v, window, dilation)
        outs.append(o)
        lses.append(lse)
    wts = jax.nn.softmax(jnp.stack(lses, axis=0), axis=0)
    return jnp.einsum('pbth,pbthc->bthc', wts, jnp.stack(outs, axis=0))


def neighbourhood_attention(q, k, v, rpb):
    B, T, H, hd = q.shape
    rows = T // GRID_W
    kh = min(NA_ROWS_MAX, rows)
    r = jnp.arange(rows)
    rs = jnp.clip(r - kh // 2, 0, rows - kh)
    row_idx = rs[:, None] + jnp.arange(kh)[None, :]
    c = jnp.arange(GRID_W)
    cs = jnp.clip(c - NA_COLS // 2, 0, GRID_W - NA_COLS)
    qg = q.reshape(B, rows, GRID_W, H, hd)
    kg = k.reshape(B, rows, GRID_W, H, hd)[:, row_idx]
    vg = v.reshape(B, rows, GRID_W, H, hd)[:, row_idx]
    s = jnp.einsum('brqhc,brawhc->brhqaw', qg, kg) * (hd ** -0.5)
    dr = row_idx - r[:, None]
    dc = c[None, :] - c[:, None]
    ridx = (dr + NA_ROWS_MAX - 1)[:, :, None, None]
    cidx = (jnp.clip(dc, -(NA_COLS - 1), NA_COLS - 1) + NA_COLS - 1)[None, None]
    bias = rpb.astype(jnp.float32)[:, ridx, cidx]
    s = s + bias.transpose(1, 0, 3, 2, 4)[None]
    col_ok = (c[None, :] >= cs[:, None]) & (c[None, :] < cs[:, None] + NA_COLS)
    s = jnp.where(col_ok[:, None, :], s, NEG_INF)
    p = jax.nn.softmax(s.reshape(s.shape[:4] + (kh * GRID_W,)), axis=-1).reshape(s.shape)
    o = jnp.einsum('brhqaw,brawhc->brqhc', p, vg)
    return o.reshape(B, T, H, hd)


def token_mixer(x, w_in, rpb, g_mix_a, g_mix_b, w_out):
    B, T, _ = x.shape
    proj = (x @ w_in).astype(jnp.float32)
    splits = [WIDTH_A, 2 * WIDTH_A, 3 * WIDTH_A, 3 * WIDTH_A + WIDTH_B, 3 * WIDTH_A + 2 * WIDTH_B]
    qa, ka, va, qb, kb, vb = jnp.split(proj, splits, axis=-1)
    qa = rotary(qa.reshape(B, T, N_HEADS_A, HEAD_DIM))
    ka = rotary(ka.reshape(B, T, N_HEADS_A, HEAD_DIM))
    va = va.reshape(B, T, N_HEADS_A, HEAD_DIM)
    oa = dilated_mixture_attention(qa, ka, va).reshape(B, T, WIDTH_A)
    ob = neighbourhood_attention(qb.reshape(B, T, N_HEADS_B, HEAD_DIM),
                                 kb.reshape(B, T, N_HEADS_B, HEAD_DIM),
                                 vb.reshape(B, T, N_HEADS_B, HEAD_DIM), rpb).reshape(B, T, WIDTH_B)
    y = jnp.concatenate([rms_norm(oa, g_mix_a), rms_norm(ob, g_mix_b)], axis=-1).astype(x.dtype)
    return y @ w_out


def memory_cross_attention(x, mem, w_xq, w_xkv, w_xo):
    B, T, D = x.shape
    M = mem.shape[1]
    q = (x @ w_xq).astype(jnp.float32).reshape(B, T, N_HEADS_X, HEAD_DIM_X)
    kv = (mem @ w_xkv).astype(jnp.float32).reshape(B, M, 2, N_HEADS_X, HEAD_DIM_X)
    k, v = kv[:, :, 0], kv[:, :, 1]
    s = jnp.einsum('bthc,bmhc->bhtm', q, k) * (HEAD_DIM_X ** -0.5)
    p = jax.nn.softmax(s, axis=-1)
    o = jnp.einsum('bhtm,bmhc->bthc', p, v).reshape(B, T, D).astype(x.dtype)
    return o @ w_xo


def squared_relu_mlp(x, w_up, w_down):
    return jnp.square(jax.nn.relu(x @ w_up)) @ w_down


def encoder_trunk(x, mem, ln_in_g, ln_in_b, w_in, rpb, g_mix_a, g_mix_b, w_out, ln1_g, ln1_b,
                  w_xq, w_xkv, w_xo, ln2_g, ln2_b, w_up, w_down, ln3_g, ln3_b):
    x = layer_norm(x, ln_in_g, ln_in_b)
    for l in range(DEPTH):
        x = layer_norm(ALPHA * x + token_mixer(x, w_in[l], rpb[l], g_mix_a[l], g_mix_b[l], w_out[l]),
                       ln1_g[l], ln1_b[l])
        x = layer_norm(ALPHA * x + memory_cross_attention(x, mem, w_xq[l], w_xkv[l], w_xo[l]),
                       ln2_g[l], ln2_b[l])
        x = layer_norm(ALPHA * x + squared_relu_mlp(x, w_up[l], w_down[l]), ln3_g[l], ln3_b[l])
    return x


def setup_inputs(seed: int = 0) -> dict:
    key = jax.random.key(seed)
    ks = jax.random.split(key, 24)
    D = D_MODEL

    def nrm(k, shape, scale):
        return jax.random.normal(k, shape, jnp.float32) * scale

    return {
        'x_prompt': nrm(ks[0], (BATCH, SEQ, D), 1.0),
        'x_sample': nrm(ks[1], (DEC_BATCH, DEC_SEQ, D), 1.0),
        'mem_prompt': nrm(ks[2], (BATCH, N_MEM, D), 1.0),
        'mem_sample': nrm(ks[3], (DEC_BATCH, N_MEM, D), 1.0),
        'ln_in_g': 1.0 + nrm(ks[4], (D,), 0.05),
        'ln_in_b': nrm(ks[5], (D,), 0.02),
        'w_in': nrm(ks[6], (DEPTH, D, 3 * MIX_WIDTH), D ** -0.5),
        'rpb': nrm(ks[7], (DEPTH, N_HEADS_B, 2 * NA_ROWS_MAX - 1, 2 * NA_COLS - 1), 0.5),
        'g_mix_a': 1.0 + nrm(ks[8], (DEPTH, WIDTH_A), 0.05),
        'g_mix_b': 1.0 + nrm(ks[9], (DEPTH, WIDTH_B), 0.05),
        'w_out': nrm(ks[10], (DEPTH, MIX_WIDTH, D), BETA * MIX_WIDTH ** -0.5),
        'ln1_g': 1.0 + nrm(ks[11], (DEPTH, D), 0.05),
        'ln1_b': nrm(ks[12], (DEPTH, D), 0.02),
        'w_xq': nrm(ks[13], (DEPTH, D, D), D ** -0.5),
        'w_xkv': nrm(ks[14], (DEPTH, D, 2 * D), D ** -0.5),
        'w_xo': nrm(ks[15], (DEPTH, D, D), BETA * D ** -0.5),
        'ln2_g': 1.0 + nrm(ks[16], (DEPTH, D), 0.05),
        'ln2_b': nrm(ks[17], (DEPTH, D), 0.02),
        'w_up': nrm(ks[18], (DEPTH, D, D_FF), D ** -0.5),
        'w_down': nrm(ks[19], (DEPTH, D_FF, D), BETA * D_FF ** -0.5),
        'ln3_g': 1.0 + nrm(ks[20], (DEPTH, D), 0.05),
        'ln3_b': nrm(ks[21], (DEPTH, D), 0.02),
    }


def reference(x_prompt, x_sample, mem_prompt, mem_sample, ln_in_g, ln_in_b, w_in, rpb, g_mix_a,
              g_mix_b, w_out, ln1_g, ln1_b, w_xq, w_xkv, w_xo, ln2_g, ln2_b, w_up, w_down,
              ln3_g, ln3_b):
    y_prompt = encoder_trunk(x_prompt, mem_prompt, ln_in_g, ln_in_b, w_in, rpb, g_mix_a, g_mix_b,
                             w_out, ln1_g, ln1_b, w_xq, w_xkv, w_xo, ln2_g, ln2_b, w_up, w_down,
                             ln3_g, ln3_b)
    y_sample = encoder_trunk(x_sample, mem_sample, ln_in_g, ln_in_b, w_in, rpb, g_mix_a, g_mix_b,
                             w_out, ln1_g, ln1_b, w_xq, w_xkv, w_xo, ln2_g, ln2_b, w_up, w_down,
                             ln3_g, ln3_b)
    return (y_prompt, y_sample)
```

```python
import os
import numpy as np
import concourse.bass as bass
import concourse.mybir as mybir
from concourse.bass_utils import run_bass_kernel_spmd

F32 = mybir.dt.float32
BF16 = mybir.dt.bfloat16
AF = mybir.ActivationFunctionType
ALU = mybir.AluOpType
D = 1024
ALPHA = float(2.0 ** 0.25)
EPS = 1e-5
NEGM = -240000.0
PAD = 1024
NMEM = 256
SB_BASE = 16640
ENGS = ('pe', 'act', 'dve', 'pool', 'sp')


class Op:
    __slots__ = ('eng', 'fn', 'deps', 'dma', 'cost', 'lat', 'seg', 'idx', 'pos', 'succ', 'ndep', 'ready', 'done', 'dk', 'sig', 'cnt')

    def __init__(self, eng, fn, deps, dma, cost, lat, seg, idx):
        self.eng, self.fn, self.deps, self.dma, self.cost, self.lat, self.seg, self.idx = eng, fn, deps, dma, cost, lat, seg, idx
        self.sig = False


class Sched:
    NDMA = 12

    def __init__(self):
        self.all = []
        self.lastw = {}
        self.readers = {}
        self.seg = 0
        self.reorder = True

    def add(self, eng, fn, reads=(), writes=(), dma=False, cost=300.0, lat=0.0):
        deps = set()
        for r in reads:
            w = self.lastw.get(r)
            if w is not None:
                deps.add(w)
            if r.startswith('ps'):
                for x in self.readers.get(r, ()):
                    if x.eng != eng:
                        deps.add(x)
        for w_ in writes:
            w = self.lastw.get(w_)
            if w is not None:
                deps.add(w)
            deps.update(self.readers.get(w_, ()))
        op = Op(eng, fn, deps, dma, cost, lat, self.seg, len(self.all))
        self.all.append(op)
        for r in reads:
            self.readers.setdefault(r, []).append(op)
        for w_ in writes:
            self.lastw[w_] = op
            self.readers[w_] = []
        return op

    def barrier(self):
        self.seg += 1
        self.lastw.clear()
        self.readers.clear()

    def _schedule_segment(self, ops):
        import heapq
        out = {e: [] for e in ENGS}
        if not self.reorder:
            for o in ops:
                out[o.eng].append(o)
            return out
        for o in ops:
            o.succ = []
            o.ndep = 0
        for o in ops:
            for d in o.deps:
                d.succ.append(o)
                o.ndep += 1
        mode = os.environ.get('PRIO', 'bl')
        for o in reversed(ops):
            b = 0.0
            for s in o.succ:
                if s.ready > b:
                    b = s.ready
            o.ready = b + o.cost + o.lat
        for o in ops:
            o.pos = (-o.ready, o.idx) if mode == 'bl' else (o.idx, o.idx)
        free = {e: 0.0 for e in ENGS}
        ht = {e: [] for e in ENGS}
        hi = {e: [] for e in ENGS}
        for o in ops:
            if o.ndep == 0:
                o.ready = 0.0
                heapq.heappush(ht[o.eng], (0.0, o.pos, o))
        left = len(ops)
        while left:
            best = None
            for e in ENGS:
                t_, i_ = ht[e], hi[e]
                while t_ and t_[0][0] <= free[e]:
                    r, ix, o = heapq.heappop(t_)
                    heapq.heappush(i_, (ix, id(o), o))
                if i_:
                    st = free[e]
                elif t_:
                    st = t_[0][0]
                else:
                    continue
                if best is None or st < best[0]:
                    best = (st, e)
            st, e = best
            if hi[e]:
                ix, _, o = heapq.heappop(hi[e])
            else:
                r, ix, o = heapq.heappop(ht[e])
            free[e] = st + o.cost
            o.done = st + o.cost + o.lat
            out[e].append(o)
            left -= 1
            for s in o.succ:
                s.ndep -= 1
                if s.ndep == 0:
                    s.ready = max(d.done for d in s.deps) + float(os.environ.get("HOPLAT", "60"))
                    heapq.heappush(ht[s.eng], (s.ready, s.pos, s))
        self.est = max(self.est, max(free.values())) if hasattr(self, 'est') else max(free.values())
        return out

    def prepare(self):
        nseg = self.seg + 1
        segs = [[] for _ in range(nseg)]
        for o in self.all:
            segs[o.seg].append(o)
        self.stream = {e: [] for e in ENGS}
        self.est_total = 0.0
        for si, ops in enumerate(segs):
            if si > 0:
                prev = self.prev_sched
                deps = set()
                for e in ENGS:
                    if prev[e]:
                        deps.add(prev[e][-1])
                    for o in prev[e]:
                        if o.dma:
                            deps.add(o)
                for e in ENGS:
                    self.stream[e].append(('barrier', deps))
            if hasattr(self, 'est'):
                del self.est
            sched = self._schedule_segment(ops)
            self.est_total += getattr(self, 'est', 0.0)
            if os.environ.get('SCHED_V'):
                print(f"[seg {si}] n={len(ops)} est_us={getattr(self, 'est', 0.0) / 1000:.0f} busy_us=",
                      {e: round(sum(o.cost for o in sched[e]) / 1000) for e in ENGS}, flush=True)
            self.prev_sched = sched
            for e in ENGS:
                self.stream[e].extend(sched[e])
        self.ndma_q = {e: 0 for e in ENGS}
        for e in ENGS:
            for p, o in enumerate(self.stream[e]):
                if isinstance(o, Op):
                    o.pos = p
                    if o.dma:
                        o.dk = self.ndma_q[e]
                        self.ndma_q[e] += 1
        for e in ENGS:
            for p, o in enumerate(self.stream[e]):
                for d in self._eff_deps(e, p, o):
                    if not d.dma:
                        d.sig = True
        for e in ENGS:
            c = 0
            for o in self.stream[e]:
                if isinstance(o, Op):
                    if o.sig:
                        c += 1
                    o.cnt = c

    def _eff_deps(self, e, p, o):
        deps = o.deps if isinstance(o, Op) else o[1]
        last = {}
        res = []
        for d in deps:
            if not self._needs_wait(e, p, d):
                continue
            if d.dma:
                res.append(d)
            else:
                cur = last.get(d.eng)
                if cur is None or d.pos > cur.pos:
                    last[d.eng] = d
        res.extend(last.values())
        return res

    def _needs_wait(self, e, p, d):
        if d.dma:
            return True
        if d.eng != e:
            return True
        if e == 'pe':
            return False
        return True

    def emit_engine(self, e, eng, sems, dsems):
        waited = {}

        def wait(key, sem, val):
            if waited.get(key, 0) >= val:
                return
            waited[key] = val
            eng.wait_ge(sem, val)

        for p, o in enumerate(self.stream[e]):
            for d in sorted(self._eff_deps(e, p, o), key=lambda x: x.idx):
                if d.dma:
                    s = d.dk % self.NDMA
                    wait(('d', d.eng, s), dsems[d.eng][s], 16 * (d.dk // self.NDMA + 1))
                else:
                    wait(('e', d.eng), sems[d.eng], d.cnt)
            if not isinstance(o, Op):
                continue
            if o.dma:
                s = o.dk % self.NDMA
                if o.dk >= self.NDMA:
                    wait(('d', e, s), dsems[e][s], 16 * (o.dk // self.NDMA))
                o.fn(eng).then_inc(dsems[e][s], 16)
            else:
                ins = o.fn(eng)
                if o.sig:
                    ins.then_inc(sems[e], 1)
        if e == 'sp':
            for q in ENGS:
                n = self.ndma_q[q]
                for s in range(min(n, self.NDMA)):
                    cntq = (n - s + self.NDMA - 1) // self.NDMA
                    eng.wait_ge(dsems[q][s], 16 * cntq)


def _bias_tables():
    R = 64
    M = R // 2
    cfgs = [(10, 8 + j) for j in range(5)]
    for m in (0, 1):
        cfgs += [(m, j) for j in range(4)]
    for m in (M - 2, M - 1):
        cfgs += [(m, M - 4 + j) for j in range(4)]
    ridx = np.zeros((21, 128, 128), np.int64)
    cidx = np.zeros((21, 128, 128), np.int64)
    mask = np.zeros((21, 128, 128), np.float32)
    k = np.arange(128)
    krl, kc = k // 64, k % 64
    qrl, qc = k // 64, k % 64
    cs = np.clip(qc - 8, 0, 48)
    for ci, (m, kt) in enumerate(cfgs):
        kr = (2 * kt + krl)[:, None]
        qr = (2 * m + qrl)[None, :]
        rs = np.clip(qr - 4, 0, R - 8)
        rok = (kr >= rs) & (kr < rs + 8)
        cok = (kc[:, None] >= cs[None, :]) & (kc[:, None] < cs[None, :] + 16)
        ridx[ci] = np.clip(kr - qr + 7, 0, 14)
        cidx[ci] = np.clip(kc[:, None] - qc[None, :], -15, 15) + 15
        mask[ci] = np.where(rok & cok, 0.0, NEGM)
    return ridx, cidx, mask


def _consts(Tmax):
    half = 32
    inv = (10000.0 ** (-np.arange(half, dtype=np.float32) / half)).astype(np.float32)
    ang = np.arange(Tmax, dtype=np.float32)[:, None] * inv[None, :]
    cos = np.cos(ang).astype(np.float32).T
    sin = np.sin(ang).astype(np.float32).T
    cosT = np.concatenate([cos, cos, cos, cos], 0)
    sinT = np.concatenate([-sin, sin, -sin, sin], 0)
    ident = np.eye(128, dtype=np.float32)
    p = np.arange(128)[:, None]
    q = np.arange(128)[None, :]
    mA = np.where(p >= q, 1.0, 0.0).astype(np.float32)
    mB = np.where(p <= q, 1.0, 0.0).astype(np.float32)
    band = np.concatenate([mA, mB, mA, mB], 1)
    m = np.arange(128)
    perm = np.zeros((128, 128), np.float32)
    perm[(m // 64) * 64 + ((m % 64) + 32) % 64, m] = 1.0
    return np.ascontiguousarray(cosT), np.ascontiguousarray(sinT), ident, np.ascontiguousarray(band), perm


class Builder:
    def __init__(self, seqs, debug=False, phases=5):
        self.seqs = list(seqs)
        self.Ttot = sum(seqs)
        self.Tmax = max(seqs)
        self.debug = debug
        self.phases = phases
        self.nc = bass.Bass("TRN2", target_bir_lowering=False)
        self.S = Sched()
        self.uid = 0
        self.sb_off = SB_BASE
        self.sbnames = set()
        self.fresh = [True] * 8
        self.psi = 0
        self.rr = 0
        self.stq = 'sp'

    def sb(self, name, shape, dtype):
        nbytes = int(np.prod(shape[1:])) * (2 if dtype == BF16 else 4)
        nbytes = (nbytes + 31) // 32 * 32
        self.uid += 1
        off = self.sb_off
        if off + nbytes > 229376 and os.environ.get('NOASSERT'):
            off = SB_BASE
        t = self.nc.alloc_sbuf_tensor_at(f"{name}_{self.uid}", list(shape), dtype, offset=off)
        self.sb_off += nbytes
        assert self.sb_off <= 229376 or os.environ.get('NOASSERT'), (name, self.sb_off)
        self.sbnames.add(t.name)
        return t

    def reset_sb(self):
        self.sb_off = SB_BASE

    def din(self, name, shape, dtype=F32):
        return self.nc.dram_tensor(name, list(shape), dtype, kind="ExternalInput").ap()

    def dscr(self, name, shape, dtype):
        kind = "ExternalOutput" if (self.debug and name in self.debug) else "Internal"
        return self.nc.dram_tensor(name, list(shape), dtype, kind=kind).ap()

    def keys(self, aps):
        ks = []
        for a in aps:
            n = a.tensor.name
            if n in self.sbnames:
                ks.append(n)
        return ks

    def I(self, eng, fn, outs, ins, cost=None):
        if cost is None:
            n = 1
            for s_ in outs[0].shape[1:]:
                n *= s_
            cost = {'act': 220 + n / 1.2, 'dve': 120 + n / 0.96, 'pool': 200 + n * 2.1, 'pe': 110.0}[eng]
        return self.S.add(eng, fn, self.keys(ins), self.keys(outs), cost=cost)

    def dma(self, out, in_, q='sp'):
        n = 1
        for s_ in out.shape:
            n *= s_
        nbytes = n * (2 if out.dtype == BF16 else 4)
        return self.S.add(q, lambda e: e.dma_start(out=out, in_=in_), self.keys([in_]), self.keys([out]), dma=True,
                          cost=(80.0 if q == 'sp' else 1200.0), lat=2000.0 + nbytes / 150.0)

    def bank(self, b):
        self.fresh[b] = True
        return self.ps[b]

    def mm(self, b, out, lhsT, rhs):
        st = self.fresh[b]
        self.fresh[b] = False
        n = 1
        for s_ in rhs.shape[1:]:
            n *= s_
        self.I('pe', lambda e: e.matmul(out, lhsT, rhs, start=st, stop=False, skip_group_check=True),
               [out], [lhsT, rhs], cost=max(n / 2.4 + 4.0, 100.0))

    def tr(self, out, in_):
        ident = self.identb[:]
        self.I('pe', lambda e: e.transpose(out, in_, ident), [out], [in_, ident])

    def _n(self, ap):
        n = 1
        for s_ in ap.shape[1:]:
            n *= s_
        return n

    def act(self, out, in_, func, **kw):
        extra = [v for v in kw.values() if hasattr(v, 'tensor')]
        outs = [out] + ([kw['accum_out']] if 'accum_out' in kw else [])
        n = self._n(out)
        cost = (180 + 0.75 * n) if in_.tensor.name.startswith('ps') else (200 + 1.0 * n)
        self.I('act', lambda e: e.activation(out, in_, func, **kw), outs, [in_] + extra, cost=cost)

    def tt(self, eng, out, in0, in1, op):
        cost = None
        if eng == 'dve':
            n = 1
            for s_ in out.shape[1:]:
                n *= s_
            allb = all(a.dtype == BF16 for a in (out, in0, in1))
            cost = 120 + 0.6 * n if allb else 140 + 1.3 * n
        self.I(eng, lambda e: e.tensor_tensor(out, in0, in1, op), [out], [in0, in1], cost=cost)

    def ts(self, eng, out, in0, s1, s2, op0, op1=None):
        extra = [v for v in (s1, s2) if hasattr(v, 'tensor')]
        if op1 is None:
            self.I(eng, lambda e: e.tensor_scalar(out, in0, s1, None, op0), [out], [in0] + extra)
        else:
            self.I(eng, lambda e: e.tensor_scalar(out, in0, s1, s2, op0, op1), [out], [in0] + extra)

    def stt(self, out, in0, scalar, in1, op0, op1):
        extra = [scalar] if hasattr(scalar, 'tensor') else []
        self.I('dve', lambda e: e.scalar_tensor_tensor(out, in0, scalar, in1, op0, op1), [out], [in0, in1] + extra)

    def copy(self, eng, out, in_):
        n = self._n(out)
        if eng == 'act':
            cost = (180 + 0.75 * n) if in_.tensor.name.startswith('ps') else (200 + 0.8 * n)
            self.I('act', lambda e: e.copy(out, in_), [out], [in_], cost=cost)
        else:
            cost = None
            if eng == 'dve':
                cost = (110 + 0.5 * n) if in_.dtype == BF16 else (120 + 0.85 * n)
            self.I(eng, lambda e: e.tensor_copy(out, in_), [out], [in_], cost=cost)

    def memset(self, eng, ap, v):
        self.I(eng, lambda e: e.memset(ap, v), [ap], [])

    def rot(self, engs):
        self.rr += 1
        return engs[self.rr % len(engs)]

    def bcast_row(self, dst, src_row):
        n = dst.shape[-1]
        src = bass.AP(src_row.tensor, src_row.offset, [[0, 128], [1, n]])
        self.dma(dst, src)

    def layer_norm(self, src, g_bc, b_bc, xf, xb):
        st = self.stats_l[self.lnk % 4]
        mv = self.mv_l[self.lnk % 4]
        self.lnk += 1
        self.I('dve', lambda e: e.bn_stats(st[:, 0:6], src[:, 0:512]), [st[:, 0:6]], [src])
        self.I('dve', lambda e: e.bn_stats(st[:, 6:12], src[:, 512:1024]), [st[:, 6:12]], [src])
        self.I('dve', lambda e: e.bn_aggr(mv[:, 0:2], st[:, 0:12]), [mv[:]], [st[:]])
        self.ts('pool', mv[:, 2:3], mv[:, 1:2], EPS, None, ALU.add)
        self.tt('pool', mv[:, 3:4], mv[:, 2:3], self.mhalf[:, 0:1], ALU.pow)
        self.stt(mv[:, 4:5], mv[:, 0:1], -1.0, mv[:, 3:4], ALU.mult, ALU.mult)
        self.act(xf, src, AF.Identity, bias=mv[:, 4:5], scale=mv[:, 3:4])
        self.tt('dve', xf, xf, g_bc, ALU.mult)
        if xb is not None:
            self.tt('dve', xb, xf, b_bc, ALU.add)
            self.tt('pool', xf, xf, b_bc, ALU.add)
        else:
            self.tt('dve', xf, xf, b_bc, ALU.add)

    def transpose8(self, xb, dstT, bnk, eng='dve'):
        pb = self.bank(bnk)[:].bitcast(BF16)
        for c in range(8):
            self.tr(pb[:, c * 128:(c + 1) * 128], xb[:, c * 128:(c + 1) * 128])
        self.copy(eng, dstT, pb.rearrange("p (c t) -> p c t", c=8))

    def load_w(self, dst, src, ncols, rowscale=None, stage_cols=4096):
        KC = dst.shape[1]
        for c in range(KC):
            stg = self.wstage[c % 2]
            self.dma(stg[:, 0:ncols], src[c * 128:(c + 1) * 128, :])
            eng = self.rot(('dve', 'act'))
            if rowscale is not None:
                self.ts('dve', dst[:, c, :], stg[:, 0:ncols], rowscale[:, c:c + 1], None, ALU.mult)
            else:
                self.copy(eng, dst[:, c, :], stg[:, 0:ncols])

    def build(self):
        nc = self.nc
        Ttot, nseq = self.Ttot, len(self.seqs)
        self.x = self.din("x", [Ttot, D])
        self.mem = self.din("mem", [nseq * NMEM, D])
        self.w_in = self.din("w_in", [D, 3072])
        self.w_out = self.din("w_out", [D, D])
        self.w_xq = self.din("w_xq", [D, D])
        self.w_xkv = self.din("w_xkv", [D, 2 * D])
        self.w_xo = self.din("w_xo", [D, D])
        self.w_up = self.din("w_up", [D, 4 * D])
        self.w_down = self.din("w_down", [4 * D, D])
        self.lnv = self.din("lnv", [8, D])
        self.gmix = self.din("gmix", [128, 8])
        self.biasG = self.din("biasG", [21, 8, 128, 128])
        self.maskC = self.din("maskC", [21, 128, 128])
        self.cosT = self.din("cosT", [128, self.Tmax])
        self.sinT = self.din("sinT", [128, self.Tmax])
        self.identf = self.din("ident", [128, 128])
        self.bandf = self.din("band", [128, 512])
        self.permf = self.din("perm", [128, 128])
        self.y = nc.dram_tensor("y", [Ttot, D], F32, kind="ExternalOutput").ap()
        self.QA = self.dscr("QA", [512, Ttot], BF16)
        self.KA = self.dscr("KA", [512, Ttot], BF16)
        self.QB = self.dscr("QB", [512, Ttot], BF16)
        self.KB = self.dscr("KB", [512, Ttot], BF16)
        self.VA = self.dscr("VA", [Ttot, 520], BF16)
        self.VB = self.dscr("VB", [Ttot, 520], BF16)
        self.X0 = self.dscr("X0", [Ttot, D], F32)
        self.OD = self.dscr("OD", [3, Ttot, 520], F32)
        self.OB = self.dscr("OB", [Ttot, 512], F32)
        self.X2 = self.dscr("X2", [Ttot, D], F32)
        self.WUPb = self.dscr("WUPb", [D, 4 * D], BF16)
        self.WDNb = self.dscr("WDNb", [4 * D, D], BF16)
        self.ps = [nc.alloc_psum_tensor(f"ps{i}", [128, 512], F32) for i in range(8)]
        for p in self.ps:
            self.sbnames.add(p.name)
        self.phase1()
        if self.phases >= 2:
            self.phase2a()
        if self.phases >= 3:
            self.phase2b()
        if self.phases >= 4:
            self.phase3a()
        if self.phases >= 5:
            self.phase3b()
        self.S.barrier()
        self.emit()
        return nc

    def common_consts(self):
        self.identb = self.sb("identb", [128, 128], BF16)
        idf = self.sb("identf", [128, 128], F32)
        self.dma(idf[:], self.identf)
        self.copy('dve', self.identb[:], idf[:])
        self.mhalf = self.sb("mhalf", [128, 1], F32)
        self.memset('pool', self.mhalf[:], -0.5)
        self.stats_l = [self.sb(f"stats{i}", [128, 12], F32) for i in range(4)]
        self.mv_l = [self.sb(f"mv{i}", [128, 8], F32) for i in range(4)]
        self.lnk = 0

    def seq_iter(self):
        t0 = 0
        for si, T in enumerate(self.seqs):
            yield si, t0, T
            t0 += T

    def phase1(self):
        self.S.barrier()
        self.reset_sb()
        self.common_consts()
        Win = self.sb("Win", [128, 8, 3072], BF16)
        permb = self.sb("permb", [128, 128], BF16)
        permf_ = self.sb("permf", [128, 128], F32)
        self.dma(permf_[:], self.permf)
        self.copy('dve', permb[:], permf_[:])
        qraw = [self.sb(f"qraw{i}", [128, 512], BF16) for i in range(3)]
        self.wstage = [self.sb("wst0", [128, 3072], F32), self.sb("wst1", [128, 3072], F32)]
        for c in range(8):
            stg = self.wstage[c % 2]
            self.dma(stg[:], self.w_in[c * 128:(c + 1) * 128, :])
            self.copy('dve', Win[:, c, :], stg[:])
        g_bc = self.sb("g0", [128, D], F32)
        b_bc = self.sb("b0", [128, D], F32)
        self.bcast_row(g_bc[:], self.lnv[0:1, :])
        self.bcast_row(b_bc[:], self.lnv[1:2, :])
        xt = [self.sb(f"xt{i}", [128, D], F32) for i in range(4)]
        xb = [self.sb(f"xb{i}", [128, D], BF16) for i in range(4)]
        xT = [self.sb(f"xT{i}", [128, 8, 512], BF16) for i in range(2)]
        cs = [self.sb(f"cos{i}", [128, 512], F32) for i in range(2)]
        sn = [self.sb(f"sin{i}", [128, 512], F32) for i in range(2)]
        t1 = [self.sb(f"t1_{i}", [128, 512], F32) for i in range(3)]
        t2 = [self.sb(f"t2_{i}", [128, 512], F32) for i in range(3)]
        qst = [self.sb(f"qst{i}", [128, 512], BF16) for i in range(4)]
        vst = [self.sb(f"vst{i}", [128, 8, 65], BF16) for i in range(4)]
        for v in vst:
            self.memset('pool', v[:], 1.0)
        it = 0
        nq = 0
        nv = 0
        for si, t0, T in self.seq_iter():
            for s0 in range(0, T, 512):
                sl = it % 2
                it += 1
                self.dma(cs[sl][:], self.cosT[:, s0:s0 + 512])
                self.dma(sn[sl][:], self.sinT[:, s0:s0 + 512])
                for ti in range(4):
                    g0 = t0 + s0 + ti * 128
                    x_ = xt[ti]
                    xb_ = xb[ti]
                    self.dma(x_[:], self.x[g0:g0 + 128, :])
                    self.layer_norm(x_[:], g_bc[:], b_bc[:], x_[:], xb_[:])
                    self.dma(self.X0[g0:g0 + 128, :], x_[:])
                    self.transpose8(xb_[:], xT[sl][:, :, ti * 128:(ti + 1) * 128], ti % 2)
                xTs = xT[sl]
                for f in range(8):
                    b1, b2 = 2 + (f % 2) * 2, 3 + (f % 2) * 2
                    p1 = self.bank(b1)
                    p2 = self.bank(b2)
                    for c in range(8):
                        self.mm(b1, p1[:], Win[:, c, f * 128:(f + 1) * 128], xTs[:, c, :])
                    qr_ = qraw[f % 3]
                    self.copy('act', qr_[:], p1[:])
                    self.mm(b2, p2[:], permb[:], qr_[:])
                    a, b_ = t1[f % 3], t2[f % 3]
                    self.tt('dve', a[:], p1[:], cs[sl][:], ALU.mult)
                    self.tt('dve', b_[:], p2[:], sn[sl][:], ALU.mult)
                    q_ = qst[nq % 4]
                    nq += 1
                    self.tt('pool', q_[:], a[:], b_[:], ALU.add)
                    dst = self.QA if f < 4 else self.KA
                    r0 = (f % 4) * 128
                    self.dma(dst[r0:r0 + 128, t0 + s0:t0 + s0 + 512], q_[:])
                for f in range(8):
                    bb = 6 + (f % 2)
                    p1 = self.bank(bb)
                    col = 1536 + f * 128
                    for c in range(8):
                        self.mm(bb, p1[:], Win[:, c, col:col + 128], xTs[:, c, :])
                    q_ = qst[nq % 4]
                    nq += 1
                    self.copy('act', q_[:], p1[:])
                    dst = self.QB if f < 4 else self.KB
                    r0 = (f % 4) * 128
                    self.dma(dst[r0:r0 + 128, t0 + s0:t0 + s0 + 512], q_[:])
                for ti in range(4):
                    g0 = t0 + s0 + ti * 128
                    for gi, (col, dst) in enumerate(((1024, self.VA), (2560, self.VB))):
                        bb = 2 + (nv % 4)
                        p1 = self.bank(bb)
                        for c in range(8):
                            self.mm(bb, p1[:], xTs[:, c, ti * 128:(ti + 1) * 128], Win[:, c, col:col + 512])
                        v_ = vst[nv % 4]
                        nv += 1
                        self.copy(self.rot(('act', 'dve')), v_[:, :, 0:64], p1[:].rearrange("p (h c) -> p h c", h=8))
                        self.dma(dst[g0:g0 + 128, :], v_[:].rearrange("p h c -> p (h c)"))

    def phase2a(self):
        self.S.barrier()
        self.reset_sb()
        self.common_consts()
        Tm = self.Tmax
        bandf = self.sb("bandf", [128, 512], F32)
        bandb = self.sb("bandb", [128, 512], BF16)
        self.dma(bandf[:], self.bandf)
        self.copy('dve', bandb[:], bandf[:])
        QT = [[self.sb(f"QT{i}_{h}", [128, Tm], BF16) for h in range(2)] for i in range(2)]
        KT = [self.sb(f"KT{i}", [128, Tm + 2 * PAD], BF16) for i in range(2)]
        for qq in QT:
            for q_ in qq:
                self.memset('pool', q_[:], 0.0)
        for k_ in KT:
            self.memset('pool', k_[:], 0.0)
        if os.environ.get('DBG2A') == '001':
            return
        ncmax = Tm // 128 + 16
        Vd = [self.sb(f"Vd{i}", [128, ncmax, 130], BF16) for i in range(2)]
        Pt = [self.sb(f"P{i}", [128, 512], BF16) for i in range(6)]
        pcf = self.sb("pcf", [128, 4096], F32)
        pcb = self.sb("pcb", [128, 4096], BF16)
        pc_jobs = [(self.w_up[c * 128:(c + 1) * 128, :], self.WUPb[c * 128:(c + 1) * 128, :]) for c in range(8)]
        for c in range(8):
            pc_jobs.append((self.w_down[c * 512:(c + 1) * 512, :].rearrange("(j p) n -> p j n", p=128),
                            self.WDNb[c * 512:(c + 1) * 512, :].rearrange("(j p) n -> p j n", p=128)))
        ost = [self.sb(f"ost{i}", [128, 2, 130], F32) for i in range(3)]
        ident = self.identb
        it = 0
        vi = 0
        u = 0
        og = 0
        for si, t0, T in self.seq_iter():
            for hp in range(4):
                qt = QT[it % 2]
                kt = KT[it % 2]
                it += 1
                for h in range(2):
                    self.dma(qt[h][h * 64:(h + 1) * 64, 0:T], self.QA[hp * 128 + h * 64:hp * 128 + (h + 1) * 64, t0:t0 + T])
                self.dma(kt[:, PAD:PAD + T], self.KA[hp * 128:(hp + 1) * 128, t0:t0 + T])
                if T < Tm:
                    self.memset('pool', kt[:, PAD + T:PAD + T + PAD], 0.0)
                for di, d in enumerate((1, 4, 16)):
                    n_it = len(self.seqs) * 12
                    i_it = (si * 4 + hp) * 3 + di
                    n_now = ((i_it + 1) * 16 + n_it - 1) // n_it - (i_it * 16 + n_it - 1) // n_it
                    for _ in range(n_now):
                        if pc_jobs:
                            src_, dst_ = pc_jobs.pop(0)
                            if len(src_.shape) == 3:
                                self.dma(pcf[:].rearrange("p (j n) -> p j n", j=4), src_)
                                self.copy('pool', pcb[:], pcf[:])
                                self.dma(dst_, pcb[:].rearrange("p (j n) -> p j n", j=4), q=self.stq)
                            else:
                                self.dma(pcf[:], src_)
                                self.copy('pool', pcb[:], pcf[:])
                                self.dma(dst_, pcb[:], q=self.stq)
                    if os.environ.get('DBGD') and str(d) not in os.environ['DBGD'].split(','):
                        continue
                    L = T // d
                    nb = L // 128
                    ncj = nb + 1
                    vd = Vd[vi % 2]
                    vi += 1
                    va = self.VA[t0:t0 + T, hp * 130:(hp + 1) * 130]
                    if os.environ.get('DBG2A') == '00':
                        continue
                    self.memset('pool', vd[0:64, 0:d * ncj:ncj, :], 0.0)
                    self.memset('pool', vd[64:128, nb:d * ncj:ncj, :], 0.0)
                    self.dma(vd[64:128, 0:d * ncj:ncj, :],
                             va[0:64 * d, :].rearrange("(i r) c -> i r c", r=d))
                    self.dma(vd[0:64, nb:d * ncj:ncj, :],
                             va[T - 64 * d:T, :].rearrange("(i r) c -> i r c", r=d))
                    if os.environ.get('DBG2A') == '01':
                        continue
                    if d <= 4:
                        for r in range(d):
                            sub = va.rearrange("(i r) c -> r i c", r=d)[r]
                            inner = sub[64:64 + 128 * (nb - 1), :].rearrange("(j p) c -> p j c", p=128)
                            for j0 in range(0, nb - 1, 16):
                                j1 = min(nb - 1, j0 + 16)
                                self.dma(vd[:, r * ncj + 1 + j0:r * ncj + 1 + j1, :], inner[:, j0:j1, :])
                    else:
                        for j in range(1, nb):
                            a0 = d * (64 + 128 * (j - 1))
                            self.dma(vd[:, j:d * ncj:ncj, :],
                                     va[a0:a0 + 128 * d, :].rearrange("(p r) c -> p r c", r=d))
                    odv = self.OD[di, t0:t0 + T, hp * 130:(hp + 1) * 130].rearrange("(i r) c -> r i c", r=d)
                    if os.environ.get('DBG2A') == '0':
                        continue
                    for r in range(d):
                        for b in range(nb):
                            sb_ = 2 + (u % 4)
                            u += 1
                            S_ = self.bank(sb_)
                            for h in range(2):
                                if os.environ.get('DBGH') == '0' and h == 1:
                                    continue
                                for s_, j in enumerate((b, b + 1)):
                                    if os.environ.get('DBGH') == 'n':
                                        continue
                                    c0 = PAD + r + d * (128 * j - 64)
                                    q0 = r + d * 128 * b
                                    self.mm(sb_, S_[:, (2 * h + s_) * 128:(2 * h + s_ + 1) * 128],
                                            kt[:, c0:c0 + 127 * d + 1:d],
                                            qt[h][:, q0:q0 + 127 * d + 1:d])
                            P_ = Pt[u % 6]
                            self.act(P_[:], S_[:], AF.Exp, scale=0.125)
                            self.tt('dve', P_[:], P_[:], bandb[:], ALU.mult)
                            if os.environ.get('DBG2A') == '1':
                                continue
                            ob_ = og % 2
                            if b % 2 == 0:
                                O_ = self.bank(ob_)
                            else:
                                O_ = self.ps[ob_]
                            for h in range(2):
                                for s_, j in enumerate((b, b + 1)):
                                    o0 = (b % 2) * 130 + h * 65
                                    self.mm(ob_, O_[:, o0:o0 + 65],
                                            P_[:, (2 * h + s_) * 128:(2 * h + s_ + 1) * 128],
                                            vd[:, r * ncj + j, h * 65:(h + 1) * 65])
                            if b % 2 == 1 or b == nb - 1:
                                gsz = (b % 2) + 1
                                os_ = ost[og % 3]
                                og += 1
                                self.copy('dve', os_[:, 0:gsz, :].rearrange("p a c -> p (a c)"), O_[:, 0:130 * gsz])
                                dst = odv[r].rearrange("(b q) c -> q b c", q=128)[:, b + 1 - gsz:b + 1, :]
                                self.dma(dst, os_[:, 0:gsz, :], q=self.stq)

    def phase2b(self):
        self.S.barrier()
        self.reset_sb()
        self.common_consts()
        Tm = self.Tmax
        biasT = self.sb("biasT", [128, 8, 21 * 128], BF16)
        mC = self.sb("mC", [128, 21, 128], F32)
        bst = [self.sb("bst0", [128, 21, 128], F32)] * 2
        self.dma(mC[:], self.maskC.rearrange("c k q -> k c q"))
        for h in range(8):
            st = bst[h % 2]
            self.dma(st[:], self.biasG[:, h].rearrange("c k q -> k c q"))
            stf = st[:].rearrange("p c q -> p (c q)")
            self.stt(stf, stf, 8.0, mC[:].rearrange("p c q -> p (c q)"), ALU.mult, ALU.add)
            self.act(biasT[:, h, :], stf, AF.Exp, scale=0.125)
        QT = [[self.sb(f"QT{i}_{h}", [128, Tm], BF16) for h in range(2)] for i in range(2)]
        for qq in QT:
            for q_ in qq:
                self.memset('pool', q_[:], 0.0)
        KT = [self.sb(f"KT{i}", [128, Tm], BF16) for i in range(2)]
        VBt = [self.sb(f"VB{i}", [128, Tm // 128, 130], BF16) for i in range(2)]
        P0 = [self.sb(f"Pa{i}", [128, 512], BF16) for i in range(2)]
        P1 = [self.sb(f"Pb{i}", [128, 512], BF16) for i in range(2)]
        P2 = [self.sb(f"Pc{i}", [128, 256], BF16) for i in range(2)]
        rden = [self.sb(f"rden{i}", [128, 2], F32) for i in range(2)]
        obst = [self.sb(f"obst{i}", [128, 4, 128], F32) for i in range(2)]
        ident = self.identb
        it = 0
        u = 0
        for si, t0, T in self.seq_iter():
            M = T // 128
            for hp in range(4):
                qt, kt, vb = QT[it % 2], KT[it % 2], VBt[it % 2]
                it += 1
                for h in range(2):
                    self.dma(qt[h][h * 64:(h + 1) * 64, 0:T], self.QB[hp * 128 + h * 64:hp * 128 + (h + 1) * 64, t0:t0 + T])
                self.dma(kt[:, 0:T], self.KB[hp * 128:(hp + 1) * 128, t0:t0 + T])
                vsrc = self.VB[t0:t0 + T, hp * 130:(hp + 1) * 130].rearrange("(j p) c -> p j c", p=128)
                for j0 in range(0, M, 16):
                    self.dma(vb[:, j0:j0 + 16, :], vsrc[:, j0:j0 + 16, :])
                for m in range(M):
                    if m == 0:
                        ch = [(j, 5 + j) for j in range(4)]
                    elif m == 1:
                        ch = [(j, 9 + j) for j in range(4)]
                    elif m == M - 2:
                        ch = [(M - 4 + j, 13 + j) for j in range(4)]
                    elif m == M - 1:
                        ch = [(M - 4 + j, 17 + j) for j in range(4)]
                    else:
                        ch = [(m - 2 + j, j) for j in range(5)]
                    par = u % 2
                    u += 1
                    sbk = [par * 3, par * 3 + 1, par * 3 + 2]
                    Sh = [self.bank(sbk[0]), self.bank(sbk[1])]
                    cfg0 = ch[0][1]
                    for h in range(2):
                        hg = hp * 2 + h
                        for ci in range(4):
                            ktile = ch[ci][0]
                            self.mm(sbk[h], Sh[h][:, ci * 128:(ci + 1) * 128],
                                    kt[:, ktile * 128:(ktile + 1) * 128],
                                    qt[h][:, m * 128:(m + 1) * 128])
                    Pl = [P0[par], P1[par]]
                    for h in range(2):
                        hg = hp * 2 + h
                        self.act(Pl[h][:], Sh[h][:], AF.Exp, scale=0.125)
                        self.tt('dve', Pl[h][:], Pl[h][:], biasT[:, hg, cfg0 * 128:(cfg0 + 4) * 128], ALU.mult)
                    if len(ch) == 5:
                        S2 = self.bank(sbk[2])
                        ktile = ch[4][0]
                        for h in range(2):
                            self.mm(sbk[2], S2[:, h * 128:(h + 1) * 128],
                                    kt[:, ktile * 128:(ktile + 1) * 128],
                                    qt[h][:, m * 128:(m + 1) * 128])
                        self.act(P2[par][:], S2[:, 0:256], AF.Exp, scale=0.125)
                        for h in range(2):
                            hg = hp * 2 + h
                            self.tt('dve', P2[par][:, h * 128:(h + 1) * 128], P2[par][:, h * 128:(h + 1) * 128],
                                    biasT[:, hg, 4 * 128:5 * 128], ALU.mult)
                    ob_ = 6 + par
                    O_ = self.bank(ob_)
                    for h in range(2):
                        for ci in range(len(ch)):
                            ktile = ch[ci][0]
                            lt = Pl[h][:, ci * 128:(ci + 1) * 128] if ci < 4 else P2[par][:, h * 128:(h + 1) * 128]
                            self.mm(ob_, O_[:, h * 65:(h + 1) * 65], lt, vb[:, ktile, h * 65:(h + 1) * 65])
                    Ov = O_[:, 0:130].rearrange("p (h c) -> p h c", c=65)
                    rd = rden[par]
                    self.I('dve', lambda e, o=rd[:], i=Ov[:, :, 64]: e.reciprocal(o, i), [rd[:]], [O_[:]])
                    os_ = obst[(m // 4) % 2]
                    ov = os_[:, m % 4, :].rearrange("p (h c) -> p h c", c=64)
                    self.tt('dve', ov, Ov[:, :, 0:64], rd[:].unsqueeze(2).to_broadcast([128, 2, 64]), ALU.mult)
                    if m % 4 == 3:
                        m0 = m - 3
                        dst = self.OB[t0 + m0 * 128:t0 + (m0 + 4) * 128, hp * 128:(hp + 1) * 128]
                        self.dma(dst.rearrange("(j p) c -> p j c", p=128), os_[:], q=self.stq)

    def phase3a(self):
        self.S.barrier()
        self.reset_sb()
        self.common_consts()
        gcol = self.sb("gcol", [128, 8], F32)
        self.dma(gcol[:], self.gmix)
        Wout = self.sb("Wout", [128, 8, D], BF16)
        Wxq = self.sb("Wxq", [128, 8, D], BF16)
        Wxo = self.sb("Wxo", [128, 8, D], BF16)
        gb = [self.sb(f"gb{i}", [128, D], F32) for i in range(4)]
        KmT_l = [self.sb(f"KmT{i}", [128, 8, NMEM], BF16) for i in range(len(self.seqs))]
        Vm_l = [self.sb(f"Vm{i}", [128, 2, 4 * 257], BF16) for i in range(len(self.seqs))]
        mark = self.sb_off
        self.wstage = [self.sb("wst0", [128, 2048], F32), self.sb("wst1", [128, 2048], F32)]
        Wxkv = self.sb("Wxkv", [128, 8, 2 * D], BF16)
        memT = self.sb("memT", [128, 8, NMEM], BF16)
        mf = self.sb("mf", [128, D], F32)
        mb = self.sb("mb", [128, D], BF16)
        self.sb_off = mark
        NSL = int(os.environ.get('NSLOT', '2'))
        od = [self.sb(f"od{i}", [128, 3, 520], F32) for i in range(NSL)]
        obt = [self.sb(f"obt{i}", [128, 512], F32) for i in range(NSL)]
        x0 = [self.sb(f"x0_{i}", [128, D], F32) for i in range(NSL)]
        oa_l = [self.sb(f"oa{i}", [128, 512], F32) for i in range(NSL)]
        osum_l = [self.sb(f"osum{i}", [128, 520], F32) for i in range(NSL)]
        rd8_l = [self.sb(f"rd8{i}", [128, 8], F32) for i in range(NSL)]
        ss_l = [self.sb(f"ss{i}", [128, 4], F32) for i in range(NSL)]
        junk = self.sb("junk", [128, 512], BF16)
        yb_l = [self.sb(f"yb{i}", [128, D], BF16) for i in range(NSL)]
        yT_l = [self.sb(f"yT{i}", [128, 8, 128], BF16) for i in range(NSL)]
        rr_l = [self.sb(f"rr{i}", [128, D], F32) for i in range(NSL)]
        x1_l = [self.sb(f"x1_{i}", [128, D], F32) for i in range(8)]
        x1b_l = [self.sb(f"x1b{i}", [128, D], BF16) for i in range(NSL)]
        NX = int(os.environ.get("NX", "1"))
        x1T_l = [self.sb(f"x1T{i}", [128, 8, 512], BF16) for i in range(NX)]
        qxT_l = [self.sb(f"qxT{i}", [128, 8, 512], BF16) for i in range(NX)]
        Px_l = [self.sb(f"Px{i}", [128, 512], BF16) for i in range(8 * NX)]
        ox_l = [self.sb(f"ox{i}", [128, D], BF16) for i in range(NSL)]
        oT_l = [self.sb(f"oT{i}", [128, 8, 128], BF16) for i in range(NSL)]
        rdx_l = [self.sb(f"rdx{i}", [128, 1], F32) for i in range(4)]
        self.load_w(Wout, self.w_out, D, rowscale=gcol)
        self.load_w(Wxq, self.w_xq, D)
        self.load_w(Wxo, self.w_xo, D)
        for i in range(4):
            self.bcast_row(gb[i][:], self.lnv[2 + i:3 + i, :])
        self.S.barrier()
        self.load_w(Wxkv, self.w_xkv, 2 * D)
        for si, t0, T in self.seq_iter():
            KmT, Vm = KmT_l[si], Vm_l[si]
            for mc in range(2):
                self.dma(mf[:], self.mem[si * NMEM + mc * 128:si * NMEM + (mc + 1) * 128, :])
                self.copy('act', mb[:], mf[:])
                self.transpose8(mb[:], memT[:, :, mc * 128:(mc + 1) * 128], 0)
            for f in range(8):
                bb = 2 + f % 2
                p1 = self.bank(bb)
                for c in range(8):
                    self.mm(bb, p1[:, 0:NMEM], Wxkv[:, c, f * 128:(f + 1) * 128], memT[:, c, :])
                self.copy('act', KmT[:, f, :], p1[:, 0:NMEM])
            self.memset('pool', Vm[:], 1.0)
            for mc in range(2):
                for n in range(2):
                    bb = 4 + n
                    p1 = self.bank(bb)
                    for c in range(8):
                        self.mm(bb, p1[:], memT[:, c, mc * 128:(mc + 1) * 128],
                                Wxkv[:, c, D + n * 512:D + (n + 1) * 512])
                    dstv = Vm[:, mc, n * 514:(n + 1) * 514].rearrange("p (h c) -> p h c", c=257)[:, :, 0:256]
                    self.copy('dve', dstv, p1[:].rearrange("p (h c) -> p h c", c=256))
        self.S.barrier()
        for si, t0, T in self.seq_iter():
            KmT, Vm = KmT_l[si], Vm_l[si]
            ntile = T // 128

            def load(k):
                g0 = t0 + k * 128
                sl = k % NSL
                self.dma(od[sl][:], self.OD[:, g0:g0 + 128, :].rearrange("d p c -> p d c"))
                self.dma(obt[sl][:], self.OB[g0:g0 + 128, :])
                self.dma(x0[sl][:], self.X0[g0:g0 + 128, :])

            load(0)
            for s0 in range(0, T, 512):
                sx = (s0 // 512) % NX
                x1T, qxT, Px = x1T_l[sx], qxT_l[sx], Px_l[sx * 8:sx * 8 + 8]
                for ti in range(4):
                    k = s0 // 128 + ti
                    g0 = t0 + k * 128
                    sl = k % NSL
                    if k + 1 < ntile:
                        load(k + 1)
                    od_, ob_, x0_ = od[sl], obt[sl], x0[sl]
                    oa, osum, rd8, ss, yb, yT, rr_, x1b = oa_l[sl], osum_l[sl], rd8_l[sl], ss_l[sl], yb_l[sl], yT_l[sl], rr_l[sl], x1b_l[sl]
                    x1 = x1_l[((s0 // 512) % 2) * 4:((s0 // 512) % 2) * 4 + 4]
                    self.tt('pool', osum[:], od_[:, 0, :], od_[:, 1, :], ALU.add)
                    self.tt('pool', osum[:], osum[:], od_[:, 2, :], ALU.add)
                    ov = osum[:].rearrange("p (h c) -> p h c", c=65)
                    self.I('dve', lambda e, o=rd8[:], i=ov[:, :, 64]: e.reciprocal(o, i), [rd8[:]], [osum[:]])
                    self.tt('dve', oa[:].rearrange("p (h c) -> p h c", c=64), ov[:, :, 0:64],
                            rd8[:].unsqueeze(2).to_broadcast([128, 8, 64]), ALU.mult)
                    self.act(junk[:], oa[:], AF.Square, accum_out=ss[:, 0:1])
                    self.act(junk[:], ob_[:], AF.Square, accum_out=ss[:, 1:2])
                    self.ts('pool', ss[:, 2:4], ss[:, 0:2], 1.0 / 512, EPS, ALU.mult, ALU.add)
                    self.tt('pool', ss[:, 2:4], ss[:, 2:4], self.mhalf[:, 0:1].to_broadcast([128, 2]), ALU.pow)
                    self.act(yb[:, 0:512], oa[:], AF.Identity, scale=ss[:, 2:3])
                    self.act(yb[:, 512:1024], ob_[:], AF.Identity, scale=ss[:, 3:4])
                    self.transpose8(yb[:], yT[:], 0, eng='act')
                    for n in range(2):
                        bb = 2 + n
                        p1 = self.bank(bb)
                        for c in range(8):
                            self.mm(bb, p1[:], yT[:, c, :], Wout[:, c, n * 512:(n + 1) * 512])
                        self.stt(rr_[:, n * 512:(n + 1) * 512], x0_[:, n * 512:(n + 1) * 512], ALPHA, p1[:],
                                 ALU.mult, ALU.add)
                    x1_ = x1[ti]
                    self.layer_norm(rr_[:], gb[0][:], gb[1][:], x1_[:], x1b[:])
                    self.transpose8(x1b[:], x1T[:, :, ti * 128:(ti + 1) * 128], 1, eng='act')
                for f in range(8):
                    bb = 4 + f % 2
                    p1 = self.bank(bb)
                    for c in range(8):
                        self.mm(bb, p1[:], Wxq[:, c, f * 128:(f + 1) * 128], x1T[:, c, :])
                    self.copy(self.rot(('act', 'dve')), qxT[:, f, :], p1[:])
                for hx in range(4):
                    for mc in range(2):
                        bb = 6 + mc
                        p1 = self.bank(bb)
                        for kc in range(2):
                            self.mm(bb, p1[:], KmT[:, hx * 2 + kc, mc * 128:(mc + 1) * 128], qxT[:, hx * 2 + kc, :])
                        self.act(Px[hx * 2 + mc][:], p1[:], AF.Exp, scale=1.0 / 16)
                for ti in range(4):
                    g0 = t0 + s0 + ti * 128
                    ox, oT, rr_ = ox_l[ti % NSL], oT_l[ti % NSL], rr_l[ti % NSL]
                    for hx in range(4):
                        rdx = rdx_l[hx]
                        bb = 4 + hx % 2
                        p1 = self.bank(bb)
                        for mc in range(2):
                            self.mm(bb, p1[:, 0:257], Px[hx * 2 + mc][:, ti * 128:(ti + 1) * 128],
                                    Vm[:, mc, hx * 257:(hx + 1) * 257])
                        self.I('dve', lambda e, o=rdx[:], i=p1[:, 256:257]: e.reciprocal(o, i), [rdx[:]], [p1[:]])
                        self.act(ox[:, hx * 256:(hx + 1) * 256], p1[:, 0:256], AF.Identity, scale=rdx[:, 0:1])
                    self.transpose8(ox[:], oT[:], 4, eng='act')
                    x1_ = x1[ti]
                    for n in range(2):
                        bb = 6 + n
                        p1 = self.bank(bb)
                        for c in range(8):
                            self.mm(bb, p1[:], oT[:, c, :], Wxo[:, c, n * 512:(n + 1) * 512])
                        self.stt(rr_[:, n * 512:(n + 1) * 512], x1_[:, n * 512:(n + 1) * 512], ALPHA, p1[:],
                                 ALU.mult, ALU.add)
                    x2_ = x1_
                    self.layer_norm(rr_[:], gb[2][:], gb[3][:], x2_[:], None)
                    self.dma(self.X2[g0:g0 + 128, :], x2_[:])

    def phase3b(self):
        self.S.barrier()
        self.reset_sb()
        self.common_consts()
        Wup = self.sb("Wup", [128, 8, 4 * D], BF16)
        Wdn = self.sb("Wdn", [128, 32, D], BF16)
        gb = [self.sb(f"gb{i}", [128, D], F32) for i in range(2)]
        mark = self.sb_off
        self.wstage = [self.sb("wst0", [128, 4096], F32), self.sb("wst1", [128, 4096], F32)]
        self.sb_off = mark
        NS = 256
        x2 = [self.sb(f"x2_{i}", [128, D], F32) for i in range(4)]
        x2b_l = [self.sb(f"x2b{i}", [128, D], BF16) for i in range(2)]
        x2T = [self.sb(f"x2T{i}", [128, 8, NS], BF16) for i in range(2)]
        hT = self.sb("hT", [128, 32, NS], BF16)
        rl = [self.sb(f"rl{i}", [128, NS], F32) for i in range(4)]
        rr_l = [self.sb(f"rr{i}", [128, D], F32) for i in range(2)]
        yo = [self.sb(f"yo{i}", [128, D], F32) for i in range(2)]
        for c in range(8):
            self.dma(Wup[:, c, :], self.WUPb[c * 128:(c + 1) * 128, :])
        for c in range(8):
            self.dma(Wdn[:, c * 4:(c + 1) * 4, :], self.WDNb[c * 512:(c + 1) * 512, :].rearrange("(j p) n -> p j n", p=128))
        for i in range(2):
            self.bcast_row(gb[i][:], self.lnv[6 + i:7 + i, :])
        self.S.barrier()
        items = [(t0 + s0) for si, t0, T in self.seq_iter() for s0 in range(0, T, NS)]

        def load(i):
            for ti in range(NS // 128):
                g0 = items[i] + ti * 128
                self.dma(x2[(i % 2) * 2 + ti][:], self.X2[g0:g0 + 128, :])

        load(0)
        ny = 0
        for i in range(len(items)):
            sl = i % 2
            if i + 1 < len(items):
                load(i + 1)
            for ti in range(NS // 128):
                x2_ = x2[sl * 2 + ti]
                x2b = x2b_l[ti % 2]
                self.copy('act', x2b[:], x2_[:])
                self.transpose8(x2b[:], x2T[sl][:, :, ti * 128:(ti + 1) * 128], ti % 2)
            for f in range(32):
                bb = (2, 3, 6, 7)[f % 4]
                p1 = self.bank(bb)
                for c in range(8):
                    self.mm(bb, p1[:, 0:NS], Wup[:, c, f * 128:(f + 1) * 128], x2T[sl][:, c, :])
                r_ = rl[f % 4]
                self.act(r_[:], p1[:, 0:NS], AF.Relu)
                self.tt('dve', hT[:, f, :], r_[:], r_[:], ALU.mult)
            for ti in range(NS // 128):
                g0 = items[i] + ti * 128
                x2_ = x2[sl * 2 + ti]
                rr_ = rr_l[ti % 2]
                for n in range(2):
                    bb = 4 + n
                    p1 = self.bank(bb)
                    for f in range(32):
                        self.mm(bb, p1[:], hT[:, f, ti * 128:(ti + 1) * 128], Wdn[:, f, n * 512:(n + 1) * 512])
                    self.stt(rr_[:, n * 512:(n + 1) * 512], x2_[:, n * 512:(n + 1) * 512], ALPHA, p1[:],
                             ALU.mult, ALU.add)
                y_ = yo[ny % 2]
                ny += 1
                self.layer_norm(rr_[:], gb[0][:], gb[1][:], y_[:], None)
                self.dma(self.y[g0:g0 + 128, :], y_[:])

    def emit(self):
        nc = self.nc
        S = self.S
        S.prepare()
        sems = {e: nc.alloc_semaphore(f"s_{e}") for e in ENGS}
        dsems = {q: [nc.alloc_semaphore(f"d_{q}_{i}") for i in range(S.NDMA)] for q in ('sp', 'pool', 'act')}
        dsems['pe'] = dsems['dve'] = []
        print("[sched] ops per engine", {e: len(S.stream[e]) for e in ENGS}, "est_ms", round(S.est_total / 1e6, 3),
              "sig", {e: max([o.cnt for o in S.stream[e] if isinstance(o, Op)] or [0]) for e in ENGS}, flush=True)
        with nc.Block() as block:
            @block.tensor
            def _(eng):
                S.emit_engine('pe', eng, sems, dsems)

            @block.scalar
            def _(eng):
                S.emit_engine('act', eng, sems, dsems)

            @block.vector
            def _(eng):
                S.emit_engine('dve', eng, sems, dsems)

            @block.gpsimd
            def _(eng):
                S.emit_engine('pool', eng, sems, dsems)

            @block.sync
            def _(eng):
                S.emit_engine('sp', eng, sems, dsems)


def host_inputs(inputs, seqs_cfg):
    ridx, cidx, mask = _bias_tables()
    rpb = np.asarray(inputs['rpb'], np.float32)[0]
    biasG = np.ascontiguousarray(np.stack([rpb[h][ridx, cidx] for h in range(8)], 1))
    Tmax = max(max(c) for c in seqs_cfg)
    cosT, sinT, ident, band, perm = _consts(Tmax)
    lnv = np.ascontiguousarray(np.stack([
        np.asarray(inputs['ln_in_g']), np.asarray(inputs['ln_in_b']),
        np.asarray(inputs['ln1_g'])[0], np.asarray(inputs['ln1_b'])[0],
        np.asarray(inputs['ln2_g'])[0], np.asarray(inputs['ln2_b'])[0],
        np.asarray(inputs['ln3_g'])[0], np.asarray(inputs['ln3_b'])[0]], 0).astype(np.float32))
    gmix = np.ascontiguousarray(np.concatenate([np.asarray(inputs['g_mix_a'])[0], np.asarray(inputs['g_mix_b'])[0]]).reshape(8, 128).T)
    shared = dict(
        w_in=np.ascontiguousarray(np.asarray(inputs['w_in'])[0]), w_out=np.ascontiguousarray(np.asarray(inputs['w_out'])[0]),
        w_xq=np.ascontiguousarray(np.asarray(inputs['w_xq'])[0]), w_xkv=np.ascontiguousarray(np.asarray(inputs['w_xkv'])[0]),
        w_xo=np.ascontiguousarray(np.asarray(inputs['w_xo'])[0]), w_up=np.ascontiguousarray(np.asarray(inputs['w_up'])[0]),
        w_down=np.ascontiguousarray(np.asarray(inputs['w_down'])[0]),
        lnv=lnv, gmix=gmix, biasG=biasG, maskC=mask, cosT=cosT, sinT=sinT, ident=ident, band=band, perm=perm)
    return shared


def kernel(**inputs):
    n = 8
    xp = np.asarray(inputs['x_prompt'])
    xs = np.asarray(inputs['x_sample'])
    mp = np.asarray(inputs['mem_prompt'])
    ms = np.asarray(inputs['mem_sample'])
    Tp, Ts = xp.shape[1], xs.shape[1]
    seqs = [Tp, Tp, Ts]
    shared = host_inputs(inputs, [seqs])
    nc = Builder(seqs).build()
    in_maps = []
    for c in range(n):
        m = dict(shared)
        m['x'] = np.ascontiguousarray(np.concatenate([xp[2 * c], xp[2 * c + 1], xs[c]], 0))
        m['mem'] = np.ascontiguousarray(np.concatenate([mp[2 * c], mp[2 * c + 1], ms[c]], 0))
        in_maps.append(m)
    res = run_bass_kernel_spmd(nc, in_maps, core_ids=list(range(n)))
    yp = np.empty(xp.shape, np.float32)
    ys = np.empty(xs.shape, np.float32)
    for c in range(n):
        y = res.results[c]['y']
        yp[2 * c] = y[0:Tp]
        yp[2 * c + 1] = y[Tp:2 * Tp]
        ys[c] = y[2 * Tp:2 * Tp + Ts]
    return (yp, ys)
```

```python
import os
import numpy as np
import concourse.bass as bass
import concourse.mybir as mybir
from concourse.bass_utils import run_bass_kernel_spmd

F32 = mybir.dt.float32
BF16 = mybir.dt.bfloat16
AF = mybir.ActivationFunctionType
ALU = mybir.AluOpType
D = 1024
ALPHA = float(2.0 ** 0.25)
EPS = 1e-5
NEGM = -240000.0
PAD = 1024
NMEM = 256
SB_BASE = 16640
ENGS = ('pe', 'act', 'dve', 'pool', 'sp')


class Op:
    __slots__ = ('eng', 'fn', 'deps', 'dma', 'cost', 'lat', 'seg', 'idx', 'pos', 'succ', 'ndep', 'ready', 'done', 'dk', 'sig', 'cnt')

    def __init__(self, eng, fn, deps, dma, cost, lat, seg, idx):
        self.eng, self.fn, self.deps, self.dma, self.cost, self.lat, self.seg, self.idx = eng, fn, deps, dma, cost, lat, seg, idx
        self.sig = False


class Sched:
    NDMA = 12

    def __init__(self):
        self.all = []
        self.lastw = {}
        self.readers = {}
        self.seg = 0
        self.reorder = True

    def add(self, eng, fn, reads=(), writes=(), dma=False, cost=300.0, lat=0.0):
        deps = set()
        for r in reads:
            w = self.lastw.get(r)
            if w is not None:
                deps.add(w)
            if r.startswith('ps'):
                for x in self.readers.get(r, ()):
                    if x.eng != eng:
                        deps.add(x)
        for w_ in writes:
            w = self.lastw.get(w_)
            if w is not None:
                deps.add(w)
            deps.update(self.readers.get(w_, ()))
        op = Op(eng, fn, deps, dma, cost, lat, self.seg, len(self.all))
        self.all.append(op)
        for r in reads:
            self.readers.setdefault(r, []).append(op)
        for w_ in writes:
            self.lastw[w_] = op
            self.readers[w_] = []
        return op

    def barrier(self):
        self.seg += 1
        self.lastw.clear()
        self.readers.clear()

    def _schedule_segment(self, ops):
        import heapq
        out = {e: [] for e in ENGS}
        if not self.reorder:
            for o in ops:
                out[o.eng].append(o)
            return out
        for o in ops:
            o.succ = []
            o.ndep = 0
        for o in ops:
            for d in o.deps:
                d.succ.append(o)
                o.ndep += 1
        mode = os.environ.get('PRIO', 'bl')
        for o in reversed(ops):
            b = 0.0
            for s in o.succ:
                if s.ready > b:
                    b = s.ready
            o.ready = b + o.cost + o.lat
        for o in ops:
            o.pos = (-o.ready, o.idx) if mode == 'bl' else (o.idx, o.idx)
        free = {e: 0.0 for e in ENGS}
        ht = {e: [] for e in ENGS}
        hi = {e: [] for e in ENGS}
        for o in ops:
            if o.ndep == 0:
                o.ready = 0.0
                heapq.heappush(ht[o.eng], (0.0, o.pos, o))
        left = len(ops)
        while left:
            best = None
            for e in ENGS:
                t_, i_ = ht[e], hi[e]
                while t_ and t_[0][0] <= free[e]:
                    r, ix, o = heapq.heappop(t_)
                    heapq.heappush(i_, (ix, id(o), o))
                if i_:
                    st = free[e]
                elif t_:
                    st = t_[0][0]
                else:
                    continue
                if best is None or st < best[0]:
                    best = (st, e)
            st, e = best
            if hi[e]:
                ix, _, o = heapq.heappop(hi[e])
            else:
                r, ix, o = heapq.heappop(ht[e])
            free[e] = st + o.cost
            o.done = st + o.cost + o.lat
            out[e].append(o)
            left -= 1
            for s in o.succ:
                s.ndep -= 1
                if s.ndep == 0:
                    s.ready = max(d.done for d in s.deps) + float(os.environ.get("HOPLAT", "0"))
                    heapq.heappush(ht[s.eng], (s.ready, s.pos, s))
        self.est = max(self.est, max(free.values())) if hasattr(self, 'est') else max(free.values())
        return out

    def prepare(self):
        nseg = self.seg + 1
        segs = [[] for _ in range(nseg)]
        for o in self.all:
            segs[o.seg].append(o)
        self.stream = {e: [] for e in ENGS}
        self.est_total = 0.0
        for si, ops in enumerate(segs):
            if si > 0:
                prev = self.prev_sched
                deps = set()
                for e in ENGS:
                    if prev[e]:
                        deps.add(prev[e][-1])
                    for o in prev[e]:
                        if o.dma:
                            deps.add(o)
                for e in ENGS:
                    self.stream[e].append(('barrier', deps))
            if hasattr(self, 'est'):
                del self.est
            sched = self._schedule_segment(ops)
            self.est_total += getattr(self, 'est', 0.0)
            if os.environ.get('SCHED_V'):
                print(f"[seg {si}] n={len(ops)} est_us={getattr(self, 'est', 0.0) / 1000:.0f} busy_us=",
                      {e: round(sum(o.cost for o in sched[e]) / 1000) for e in ENGS}, flush=True)
            self.prev_sched = sched
            for e in ENGS:
                self.stream[e].extend(sched[e])
        self.ndma_q = {e: 0 for e in ENGS}
        for e in ENGS:
            for p, o in enumerate(self.stream[e]):
                if isinstance(o, Op):
                    o.pos = p
                    if o.dma:
                        o.dk = self.ndma_q[e]
                        self.ndma_q[e] += 1
        for e in ENGS:
            for p, o in enumerate(self.stream[e]):
                for d in self._eff_deps(e, p, o):
                    if not d.dma:
                        d.sig = True
        for e in ENGS:
            c = 0
            for o in self.stream[e]:
                if isinstance(o, Op):
                    if o.sig:
                        c += 1
                    o.cnt = c

    def _eff_deps(self, e, p, o):
        deps = o.deps if isinstance(o, Op) else o[1]
        last = {}
        res = []
        for d in deps:
            if not self._needs_wait(e, p, d):
                continue
            if d.dma:
                res.append(d)
            else:
                cur = last.get(d.eng)
                if cur is None or d.pos > cur.pos:
                    last[d.eng] = d
        res.extend(last.values())
        return res

    def _needs_wait(self, e, p, d):
        if d.dma:
            return True
        if d.eng != e:
            return True
        if e == 'pe':
            return False
        return True

    def emit_engine(self, e, eng, sems, dsems):
        waited = {}

        def wait(key, sem, val):
            if waited.get(key, 0) >= val:
                return
            waited[key] = val
            eng.wait_ge(sem, val)

        for p, o in enumerate(self.stream[e]):
            for d in sorted(self._eff_deps(e, p, o), key=lambda x: x.idx):
                if d.dma:
                    s = d.dk % self.NDMA
                    wait(('d', d.eng, s), dsems[d.eng][s], 16 * (d.dk // self.NDMA + 1))
                else:
                    wait(('e', d.eng), sems[d.eng], d.cnt)
            if not isinstance(o, Op):
                continue
            if o.dma:
                s = o.dk % self.NDMA
                if o.dk >= self.NDMA:
                    wait(('d', e, s), dsems[e][s], 16 * (o.dk // self.NDMA))
                o.fn(eng).then_inc(dsems[e][s], 16)
            else:
                ins = o.fn(eng)
                if o.sig:
                    ins.then_inc(sems[e], 1)
        if e == 'sp':
            for q in ENGS:
                n = self.ndma_q[q]
                for s in range(min(n, self.NDMA)):
                    cntq = (n - s + self.NDMA - 1) // self.NDMA
                    eng.wait_ge(dsems[q][s], 16 * cntq)


def _bias_tables():
    R = 64
    M = R // 2
    cfgs = [(10, 8 + j) for j in range(5)]
    for m in (0, 1):
        cfgs += [(m, j) for j in range(4)]
    for m in (M - 2, M - 1):
        cfgs += [(m, M - 4 + j) for j in range(4)]
    ridx = np.zeros((21, 128, 128), np.int64)
    cidx = np.zeros((21, 128, 128), np.int64)
    mask = np.zeros((21, 128, 128), np.float32)
    k = np.arange(128)
    krl, kc = k // 64, k % 64
    qrl, qc = k // 64, k % 64
    cs = np.clip(qc - 8, 0, 48)
    for ci, (m, kt) in enumerate(cfgs):
        kr = (2 * kt + krl)[:, None]
        qr = (2 * m + qrl)[None, :]
        rs = np.clip(qr - 4, 0, R - 8)
        rok = (kr >= rs) & (kr < rs + 8)
        cok = (kc[:, None] >= cs[None, :]) & (kc[:, None] < cs[None, :] + 16)
        ridx[ci] = np.clip(kr - qr + 7, 0, 14)
        cidx[ci] = np.clip(kc[:, None] - qc[None, :], -15, 15) + 15
        mask[ci] = np.where(rok & cok, 0.0, NEGM)
    return ridx, cidx, mask


def _consts(Tmax):
    half = 32
    inv = (10000.0 ** (-np.arange(half, dtype=np.float32) / half)).astype(np.float32)
    ang = np.arange(Tmax, dtype=np.float32)[:, None] * inv[None, :]
    cos = np.cos(ang).astype(np.float32).T
    sin = np.sin(ang).astype(np.float32).T
    cosT = np.concatenate([cos, cos, cos, cos], 0)
    sinT = np.concatenate([-sin, sin, -sin, sin], 0)
    ident = np.eye(128, dtype=np.float32)
    p = np.arange(128)[:, None]
    q = np.arange(128)[None, :]
    mA = np.where(p >= q, 1.0, 0.0).astype(np.float32)
    mB = np.where(p <= q, 1.0, 0.0).astype(np.float32)
    band = np.concatenate([mA, mB, mA, mB], 1)
    m = np.arange(128)
    perm = np.zeros((128, 128), np.float32)
    perm[(m // 64) * 64 + ((m % 64) + 32) % 64, m] = 1.0
    return np.ascontiguousarray(cosT), np.ascontiguousarray(sinT), ident, np.ascontiguousarray(band), perm


class Builder:
    def __init__(self, seqs, debug=False, phases=5):
        self.seqs = list(seqs)
        self.Ttot = sum(seqs)
        self.Tmax = max(seqs)
        self.debug = debug
        self.phases = phases
        self.nc = bass.Bass("TRN2", target_bir_lowering=False)
        self.S = Sched()
        self.uid = 0
        self.sb_off = SB_BASE
        self.sbnames = set()
        self.fresh = [True] * 8
        self.psi = 0
        self.rr = 0
        self.stq = 'sp'

    def sb(self, name, shape, dtype):
        nbytes = int(np.prod(shape[1:])) * (2 if dtype == BF16 else 4)
        nbytes = (nbytes + 31) // 32 * 32
        self.uid += 1
        off = self.sb_off
        if off + nbytes > 229376 and os.environ.get('NOASSERT'):
            off = SB_BASE
        t = self.nc.alloc_sbuf_tensor_at(f"{name}_{self.uid}", list(shape), dtype, offset=off)
        self.sb_off += nbytes
        assert self.sb_off <= 229376 or os.environ.get('NOASSERT'), (name, self.sb_off)
        self.sbnames.add(t.name)
        return t

    def reset_sb(self):
        self.sb_off = SB_BASE

    def din(self, name, shape, dtype=F32):
        return self.nc.dram_tensor(name, list(shape), dtype, kind="ExternalInput").ap()

    def dscr(self, name, shape, dtype):
        kind = "ExternalOutput" if (self.debug and name in self.debug) else "Internal"
        return self.nc.dram_tensor(name, list(shape), dtype, kind=kind).ap()

    def keys(self, aps):
        ks = []
        for a in aps:
            n = a.tensor.name
            if n in self.sbnames:
                ks.append(n)
        return ks

    def I(self, eng, fn, outs, ins, cost=None):
        if cost is None:
            n = 1
            for s_ in outs[0].shape[1:]:
                n *= s_
            cost = {'act': 220 + n / 1.2, 'dve': 120 + n / 0.96, 'pool': 200 + n * 2.1, 'pe': 110.0}[eng]
        return self.S.add(eng, fn, self.keys(ins), self.keys(outs), cost=cost)

    def dma(self, out, in_, q='sp'):
        n = 1
        for s_ in out.shape:
            n *= s_
        nbytes = n * (2 if out.dtype == BF16 else 4)
        return self.S.add(q, lambda e: e.dma_start(out=out, in_=in_), self.keys([in_]), self.keys([out]), dma=True,
                          cost=(80.0 if q == 'sp' else 1200.0), lat=2000.0 + nbytes / 150.0)

    def bank(self, b):
        self.fresh[b] = True
        return self.ps[b]

    def mm(self, b, out, lhsT, rhs):
        st = self.fresh[b]
        self.fresh[b] = False
        n = 1
        for s_ in rhs.shape[1:]:
            n *= s_
        self.I('pe', lambda e: e.matmul(out, lhsT, rhs, start=st, stop=False, skip_group_check=True),
               [out], [lhsT, rhs], cost=max(n / 2.4 + 4.0, 100.0))

    def tr(self, out, in_):
        ident = self.identb[:]
        self.I('pe', lambda e: e.transpose(out, in_, ident), [out], [in_, ident])

    def _n(self, ap):
        n = 1
        for s_ in ap.shape[1:]:
            n *= s_
        return n

    def act(self, out, in_, func, **kw):
        extra = [v for v in kw.values() if hasattr(v, 'tensor')]
        outs = [out] + ([kw['accum_out']] if 'accum_out' in kw else [])
        n = self._n(out)
        cost = (180 + 0.75 * n) if in_.tensor.name.startswith('ps') else (200 + 1.0 * n)
        self.I('act', lambda e: e.activation(out, in_, func, **kw), outs, [in_] + extra, cost=cost)

    def tt(self, eng, out, in0, in1, op):
        cost = None
        if eng == 'dve':
            n = 1
            for s_ in out.shape[1:]:
                n *= s_
            allb = all(a.dtype == BF16 for a in (out, in0, in1))
            cost = 120 + 0.6 * n if allb else 140 + 1.3 * n
        self.I(eng, lambda e: e.tensor_tensor(out, in0, in1, op), [out], [in0, in1], cost=cost)

    def ts(self, eng, out, in0, s1, s2, op0, op1=None):
        extra = [v for v in (s1, s2) if hasattr(v, 'tensor')]
        if op1 is None:
            self.I(eng, lambda e: e.tensor_scalar(out, in0, s1, None, op0), [out], [in0] + extra)
        else:
            self.I(eng, lambda e: e.tensor_scalar(out, in0, s1, s2, op0, op1), [out], [in0] + extra)

    def stt(self, out, in0, scalar, in1, op0, op1):
        extra = [scalar] if hasattr(scalar, 'tensor') else []
        self.I('dve', lambda e: e.scalar_tensor_tensor(out, in0, scalar, in1, op0, op1), [out], [in0, in1] + extra)

    def copy(self, eng, out, in_):
        n = self._n(out)
        if eng == 'act':
            cost = (180 + 0.75 * n) if in_.tensor.name.startswith('ps') else (200 + 0.8 * n)
            self.I('act', lambda e: e.copy(out, in_), [out], [in_], cost=cost)
        else:
            cost = None
            if eng == 'dve':
                cost = (110 + 0.5 * n) if in_.dtype == BF16 else (120 + 0.85 * n)
            self.I(eng, lambda e: e.tensor_copy(out, in_), [out], [in_], cost=cost)

    def memset(self, eng, ap, v):
        self.I(eng, lambda e: e.memset(ap, v), [ap], [])

    def rot(self, engs):
        self.rr += 1
        return engs[self.rr % len(engs)]

    def bcast_row(self, dst, src_row):
        n = dst.shape[-1]
        src = bass.AP(src_row.tensor, src_row.offset, [[0, 128], [1, n]])
        self.dma(dst, src)

    def layer_norm(self, src, g_bc, b_bc, xf, xb):
        st = self.stats_l[self.lnk % 4]
        mv = self.mv_l[self.lnk % 4]
        self.lnk += 1
        self.I('dve', lambda e: e.bn_stats(st[:, 0:6], src[:, 0:512]), [st[:, 0:6]], [src])
        self.I('dve', lambda e: e.bn_stats(st[:, 6:12], src[:, 512:1024]), [st[:, 6:12]], [src])
        self.I('dve', lambda e: e.bn_aggr(mv[:, 0:2], st[:, 0:12]), [mv[:]], [st[:]])
        self.ts('pool', mv[:, 2:3], mv[:, 1:2], EPS, None, ALU.add)
        self.tt('pool', mv[:, 3:4], mv[:, 2:3], self.mhalf[:, 0:1], ALU.pow)
        self.stt(mv[:, 4:5], mv[:, 0:1], -1.0, mv[:, 3:4], ALU.mult, ALU.mult)
        self.act(xf, src, AF.Identity, bias=mv[:, 4:5], scale=mv[:, 3:4])
        self.tt('dve', xf, xf, g_bc, ALU.mult)
        if xb is not None:
            self.tt('dve', xb, xf, b_bc, ALU.add)
            self.tt('pool', xf, xf, b_bc, ALU.add)
        else:
            self.tt('dve', xf, xf, b_bc, ALU.add)

    def transpose8(self, xb, dstT, bnk, eng='dve'):
        pb = self.bank(bnk)[:].bitcast(BF16)
        for c in range(8):
            self.tr(pb[:, c * 128:(c + 1) * 128], xb[:, c * 128:(c + 1) * 128])
        self.copy(eng, dstT, pb.rearrange("p (c t) -> p c t", c=8))

    def load_w(self, dst, src, ncols, rowscale=None, stage_cols=4096):
        KC = dst.shape[1]
        for c in range(KC):
            stg = self.wstage[c % 2]
            self.dma(stg[:, 0:ncols], src[c * 128:(c + 1) * 128, :])
            eng = self.rot(('dve', 'act'))
            if rowscale is not None:
                self.ts('dve', dst[:, c, :], stg[:, 0:ncols], rowscale[:, c:c + 1], None, ALU.mult)
            else:
                self.copy(eng, dst[:, c, :], stg[:, 0:ncols])

    def build(self):
        nc = self.nc
        Ttot, nseq = self.Ttot, len(self.seqs)
        self.x = self.din("x", [Ttot, D])
        self.mem = self.din("mem", [nseq * NMEM, D])
        self.w_in = self.din("w_in", [D, 3072])
        self.w_out = self.din("w_out", [D, D])
        self.w_xq = self.din("w_xq", [D, D])
        self.w_xkv = self.din("w_xkv", [D, 2 * D])
        self.w_xo = self.din("w_xo", [D, D])
        self.w_up = self.din("w_up", [D, 4 * D])
        self.w_down = self.din("w_down", [4 * D, D])
        self.lnv = self.din("lnv", [8, D])
        self.gmix = self.din("gmix", [128, 8])
        self.biasG = self.din("biasG", [21, 8, 128, 128])
        self.maskC = self.din("maskC", [21, 128, 128])
        self.cosT = self.din("cosT", [128, self.Tmax])
        self.sinT = self.din("sinT", [128, self.Tmax])
        self.identf = self.din("ident", [128, 128])
        self.bandf = self.din("band", [128, 512])
        self.permf = self.din("perm", [128, 128])
        self.y = nc.dram_tensor("y", [Ttot, D], F32, kind="ExternalOutput").ap()
        self.QA = self.dscr("QA", [512, Ttot], BF16)
        self.KA = self.dscr("KA", [512, Ttot], BF16)
        self.QB = self.dscr("QB", [512, Ttot], BF16)
        self.KB = self.dscr("KB", [512, Ttot], BF16)
        self.VA = self.dscr("VA", [Ttot, 520], BF16)
        self.VB = self.dscr("VB", [Ttot, 520], BF16)
        self.X0 = self.dscr("X0", [Ttot, D], F32)
        self.OD = self.dscr("OD", [3, Ttot, 520], F32)
        self.OB = self.dscr("OB", [Ttot, 512], F32)
        self.X2 = self.dscr("X2", [Ttot, D], F32)
        self.WUPb = self.dscr("WUPb", [D, 4 * D], BF16)
        self.WDNb = self.dscr("WDNb", [4 * D, D], BF16)
        self.ps = [nc.alloc_psum_tensor(f"ps{i}", [128, 512], F32) for i in range(8)]
        for p in self.ps:
            self.sbnames.add(p.name)
        self.phase1()
        if self.phases >= 2:
            self.phase2a()
        if self.phases >= 3:
            self.phase2b()
        if self.phases >= 4:
            self.phase3a()
        if self.phases >= 5:
            self.phase3b()
        self.S.barrier()
        self.emit()
        return nc

    def common_consts(self):
        self.identb = self.sb("identb", [128, 128], BF16)
        idf = self.sb("identf", [128, 128], F32)
        self.dma(idf[:], self.identf)
        self.copy('dve', self.identb[:], idf[:])
        self.mhalf = self.sb("mhalf", [128, 1], F32)
        self.memset('pool', self.mhalf[:], -0.5)
        self.stats_l = [self.sb(f"stats{i}", [128, 12], F32) for i in range(4)]
        self.mv_l = [self.sb(f"mv{i}", [128, 8], F32) for i in range(4)]
        self.lnk = 0

    def seq_iter(self):
        t0 = 0
        for si, T in enumerate(self.seqs):
            yield si, t0, T
            t0 += T

    def phase1(self):
        self.S.barrier()
        self.reset_sb()
        self.common_consts()
        Win = self.sb("Win", [128, 8, 3072], BF16)
        permb = self.sb("permb", [128, 128], BF16)
        permf_ = self.sb("permf", [128, 128], F32)
        self.dma(permf_[:], self.permf)
        self.copy('dve', permb[:], permf_[:])
        qraw = [self.sb(f"qraw{i}", [128, 512], BF16) for i in range(3)]
        self.wstage = [self.sb("wst0", [128, 3072], F32), self.sb("wst1", [128, 3072], F32)]
        for c in range(8):
            stg = self.wstage[c % 2]
            self.dma(stg[:], self.w_in[c * 128:(c + 1) * 128, :])
            self.copy('dve', Win[:, c, :], stg[:])
        g_bc = self.sb("g0", [128, D], F32)
        b_bc = self.sb("b0", [128, D], F32)
        self.bcast_row(g_bc[:], self.lnv[0:1, :])
        self.bcast_row(b_bc[:], self.lnv[1:2, :])
        xt = [self.sb(f"xt{i}", [128, D], F32) for i in range(4)]
        xb = [self.sb(f"xb{i}", [128, D], BF16) for i in range(4)]
        xT = [self.sb(f"xT{i}", [128, 8, 512], BF16) for i in range(2)]
        cs = [self.sb(f"cos{i}", [128, 512], F32) for i in range(2)]
        sn = [self.sb(f"sin{i}", [128, 512], F32) for i in range(2)]
        t1 = [self.sb(f"t1_{i}", [128, 512], F32) for i in range(3)]
        t2 = [self.sb(f"t2_{i}", [128, 512], F32) for i in range(3)]
        qst = [self.sb(f"qst{i}", [128, 512], BF16) for i in range(4)]
        vst = [self.sb(f"vst{i}", [128, 8, 65], BF16) for i in range(4)]
        for v in vst:
            self.memset('pool', v[:], 1.0)
        it = 0
        nq = 0
        nv = 0
        for si, t0, T in self.seq_iter():
            for s0 in range(0, T, 512):
                sl = it % 2
                it += 1
                self.dma(cs[sl][:], self.cosT[:, s0:s0 + 512])
                self.dma(sn[sl][:], self.sinT[:, s0:s0 + 512])
                for ti in range(4):
                    g0 = t0 + s0 + ti * 128
                    x_ = xt[ti]
                    xb_ = xb[ti]
                    self.dma(x_[:], self.x[g0:g0 + 128, :])
                    self.layer_norm(x_[:], g_bc[:], b_bc[:], x_[:], xb_[:])
                    self.dma(self.X0[g0:g0 + 128, :], x_[:])
                    self.transpose8(xb_[:], xT[sl][:, :, ti * 128:(ti + 1) * 128], ti % 2)
                xTs = xT[sl]
                for f in range(8):
                    b1, b2 = 2 + (f % 2) * 2, 3 + (f % 2) * 2
                    p1 = self.bank(b1)
                    p2 = self.bank(b2)
                    for c in range(8):
                        self.mm(b1, p1[:], Win[:, c, f * 128:(f + 1) * 128], xTs[:, c, :])
                    qr_ = qraw[f % 3]
                    self.copy('act', qr_[:], p1[:])
                    self.mm(b2, p2[:], permb[:], qr_[:])
                    a, b_ = t1[f % 3], t2[f % 3]
                    self.tt('dve', a[:], p1[:], cs[sl][:], ALU.mult)
                    self.tt('dve', b_[:], p2[:], sn[sl][:], ALU.mult)
                    q_ = qst[nq % 4]
                    nq += 1
                    self.tt('pool', q_[:], a[:], b_[:], ALU.add)
                    dst = self.QA if f < 4 else self.KA
                    r0 = (f % 4) * 128
                    self.dma(dst[r0:r0 + 128, t0 + s0:t0 + s0 + 512], q_[:])
                for f in range(8):
                    bb = 6 + (f % 2)
                    p1 = self.bank(bb)
                    col = 1536 + f * 128
                    for c in range(8):
                        self.mm(bb, p1[:], Win[:, c, col:col + 128], xTs[:, c, :])
                    q_ = qst[nq % 4]
                    nq += 1
                    self.copy('act', q_[:], p1[:])
                    dst = self.QB if f < 4 else self.KB
                    r0 = (f % 4) * 128
                    self.dma(dst[r0:r0 + 128, t0 + s0:t0 + s0 + 512], q_[:])
                for ti in range(4):
                    g0 = t0 + s0 + ti * 128
                    for gi, (col, dst) in enumerate(((1024, self.VA), (2560, self.VB))):
                        bb = 2 + (nv % 4)
                        p1 = self.bank(bb)
                        for c in range(8):
                            self.mm(bb, p1[:], xTs[:, c, ti * 128:(ti + 1) * 128], Win[:, c, col:col + 512])
                        v_ = vst[nv % 4]
                        nv += 1
                        self.copy(self.rot(('act', 'dve')), v_[:, :, 0:64], p1[:].rearrange("p (h c) -> p h c", h=8))
                        self.dma(dst[g0:g0 + 128, :], v_[:].rearrange("p h c -> p (h c)"))

    def phase2a(self):
        self.S.barrier()
        self.reset_sb()
        self.common_consts()
        Tm = self.Tmax
        bandf = self.sb("bandf", [128, 512], F32)
        bandb = self.sb("bandb", [128, 512], BF16)
        self.dma(bandf[:], self.bandf)
        self.copy('dve', bandb[:], bandf[:])
        QT = [[self.sb(f"QT{i}_{h}", [128, Tm], BF16) for h in range(2)] for i in range(2)]
        KT = [self.sb(f"KT{i}", [128, Tm + 2 * PAD], BF16) for i in range(2)]
        for qq in QT:
            for q_ in qq:
                self.memset('pool', q_[:], 0.0)
        for k_ in KT:
            self.memset('pool', k_[:], 0.0)
        if os.environ.get('DBG2A') == '001':
            return
        ncmax = Tm // 128 + 16
        Vd = [self.sb(f"Vd{i}", [128, ncmax, 130], BF16) for i in range(2)]
        Pt = [self.sb(f"P{i}", [128, 512], BF16) for i in range(6)]
        pcf = self.sb("pcf", [128, 4096], F32)
        pcb = self.sb("pcb", [128, 4096], BF16)
        pc_jobs = [(self.w_up[c * 128:(c + 1) * 128, :], self.WUPb[c * 128:(c + 1) * 128, :]) for c in range(8)]
        for c in range(8):
            pc_jobs.append((self.w_down[c * 512:(c + 1) * 512, :].rearrange("(j p) n -> p j n", p=128),
                            self.WDNb[c * 512:(c + 1) * 512, :].rearrange("(j p) n -> p j n", p=128)))
        ost = [self.sb(f"ost{i}", [128, 2, 130], F32) for i in range(3)]
        ident = self.identb
        it = 0
        vi = 0
        u = 0
        og = 0
        for si, t0, T in self.seq_iter():
            for hp in range(4):
                qt = QT[it % 2]
                kt = KT[it % 2]
                it += 1
                for h in range(2):
                    self.dma(qt[h][h * 64:(h + 1) * 64, 0:T], self.QA[hp * 128 + h * 64:hp * 128 + (h + 1) * 64, t0:t0 + T])
                self.dma(kt[:, PAD:PAD + T], self.KA[hp * 128:(hp + 1) * 128, t0:t0 + T])
                if T < Tm:
                    self.memset('pool', kt[:, PAD + T:PAD + T + PAD], 0.0)
                for di, d in enumerate((1, 4, 16)):
                    n_it = len(self.seqs) * 12
                    i_it = (si * 4 + hp) * 3 + di
                    n_now = ((i_it + 1) * 16 + n_it - 1) // n_it - (i_it * 16 + n_it - 1) // n_it
                    for _ in range(n_now):
                        if pc_jobs:
                            src_, dst_ = pc_jobs.pop(0)
                            if len(src_.shape) == 3:
                                self.dma(pcf[:].rearrange("p (j n) -> p j n", j=4), src_)
                                self.copy('pool', pcb[:], pcf[:])
                                self.dma(dst_, pcb[:].rearrange("p (j n) -> p j n", j=4), q=self.stq)
                            else:
                                self.dma(pcf[:], src_)
                                self.copy('pool', pcb[:], pcf[:])
                                self.dma(dst_, pcb[:], q=self.stq)
                    if os.environ.get('DBGD') and str(d) not in os.environ['DBGD'].split(','):
                        continue
                    L = T // d
                    nb = L // 128
                    ncj = nb + 1
                    vd = Vd[vi % 2]
                    vi += 1
                    va = self.VA[t0:t0 + T, hp * 130:(hp + 1) * 130]
                    if os.environ.get('DBG2A') == '00':
                        continue
                    self.memset('pool', vd[0:64, 0:d * ncj:ncj, :], 0.0)
                    self.memset('pool', vd[64:128, nb:d * ncj:ncj, :], 0.0)
                    self.dma(vd[64:128, 0:d * ncj:ncj, :],
                             va[0:64 * d, :].rearrange("(i r) c -> i r c", r=d))
                    self.dma(vd[0:64, nb:d * ncj:ncj, :],
                             va[T - 64 * d:T, :].rearrange("(i r) c -> i r c", r=d))
                    if os.environ.get('DBG2A') == '01':
                        continue
                    if d <= 4:
                        for r in range(d):
                            sub = va.rearrange("(i r) c -> r i c", r=d)[r]
                            inner = sub[64:64 + 128 * (nb - 1), :].rearrange("(j p) c -> p j c", p=128)
                            for j0 in range(0, nb - 1, 16):
                                j1 = min(nb - 1, j0 + 16)
                                self.dma(vd[:, r * ncj + 1 + j0:r * ncj + 1 + j1, :], inner[:, j0:j1, :])
                    else:
                        for j in range(1, nb):
                            a0 = d * (64 + 128 * (j - 1))
                            self.dma(vd[:, j:d * ncj:ncj, :],
                                     va[a0:a0 + 128 * d, :].rearrange("(p r) c -> p r c", r=d))
                    odv = self.OD[di, t0:t0 + T, hp * 130:(hp + 1) * 130].rearrange("(i r) c -> r i c", r=d)
                    if os.environ.get('DBG2A') == '0':
                        continue
                    for r in range(d):
                        for b in range(nb):
                            sb_ = 2 + (u % 4)
                            u += 1
                            S_ = self.bank(sb_)
                            for h in range(2):
                                if os.environ.get('DBGH') == '0' and h == 1:
                                    continue
                                for s_, j in enumerate((b, b + 1)):
                                    if os.environ.get('DBGH') == 'n':
                                        continue
                                    c0 = PAD + r + d * (128 * j - 64)
                                    q0 = r + d * 128 * b
                                    self.mm(sb_, S_[:, (2 * h + s_) * 128:(2 * h + s_ + 1) * 128],
                                            kt[:, c0:c0 + 127 * d + 1:d],
                                            qt[h][:, q0:q0 + 127 * d + 1:d])
                            P_ = Pt[u % 6]
                            self.act(P_[:], S_[:], AF.Exp, scale=0.125)
                            self.tt('dve', P_[:], P_[:], bandb[:], ALU.mult)
                            if os.environ.get('DBG2A') == '1':
                                continue
                            ob_ = og % 2
                            if b % 2 == 0:
                                O_ = self.bank(ob_)
                            else:
                                O_ = self.ps[ob_]
                            for h in range(2):
                                for s_, j in enumerate((b, b + 1)):
                                    o0 = (b % 2) * 130 + h * 65
                                    self.mm(ob_, O_[:, o0:o0 + 65],
                                            P_[:, (2 * h + s_) * 128:(2 * h + s_ + 1) * 128],
                                            vd[:, r * ncj + j, h * 65:(h + 1) * 65])
                            if b % 2 == 1 or b == nb - 1:
                                gsz = (b % 2) + 1
                                os_ = ost[og % 3]
                                og += 1
                                self.copy('dve', os_[:, 0:gsz, :].rearrange("p a c -> p (a c)"), O_[:, 0:130 * gsz])
                                dst = odv[r].rearrange("(b q) c -> q b c", q=128)[:, b + 1 - gsz:b + 1, :]
                                self.dma(dst, os_[:, 0:gsz, :], q=self.stq)

    def phase2b(self):
        self.S.barrier()
        self.reset_sb()
        self.common_consts()
        Tm = self.Tmax
        biasT = self.sb("biasT", [128, 8, 21 * 128], BF16)
        mC = self.sb("mC", [128, 21, 128], F32)
        bst = [self.sb("bst0", [128, 21, 128], F32)] * 2
        self.dma(mC[:], self.maskC.rearrange("c k q -> k c q"))
        for h in range(8):
            st = bst[h % 2]
            self.dma(st[:], self.biasG[:, h].rearrange("c k q -> k c q"))
            stf = st[:].rearrange("p c q -> p (c q)")
            self.stt(stf, stf, 8.0, mC[:].rearrange("p c q -> p (c q)"), ALU.mult, ALU.add)
            self.act(biasT[:, h, :], stf, AF.Exp, scale=0.125)
        QT = [[self.sb(f"QT{i}_{h}", [128, Tm], BF16) for h in range(2)] for i in range(2)]
        for qq in QT:
            for q_ in qq:
                self.memset('pool', q_[:], 0.0)
        KT = [self.sb(f"KT{i}", [128, Tm], BF16) for i in range(2)]
        VBt = [self.sb(f"VB{i}", [128, Tm // 128, 130], BF16) for i in range(2)]
        P0 = [self.sb(f"Pa{i}", [128, 512], BF16) for i in range(2)]
        P1 = [self.sb(f"Pb{i}", [128, 512], BF16) for i in range(2)]
        P2 = [self.sb(f"Pc{i}", [128, 256], BF16) for i in range(2)]
        rden = [self.sb(f"rden{i}", [128, 2], F32) for i in range(2)]
        obst = [self.sb(f"obst{i}", [128, 4, 128], F32) for i in range(2)]
        ident = self.identb
        it = 0
        u = 0
        for si, t0, T in self.seq_iter():
            M = T // 128
            for hp in range(4):
                qt, kt, vb = QT[it % 2], KT[it % 2], VBt[it % 2]
                it += 1
                for h in range(2):
                    self.dma(qt[h][h * 64:(h + 1) * 64, 0:T], self.QB[hp * 128 + h * 64:hp * 128 + (h + 1) * 64, t0:t0 + T])
                self.dma(kt[:, 0:T], self.KB[hp * 128:(hp + 1) * 128, t0:t0 + T])
                vsrc = self.VB[t0:t0 + T, hp * 130:(hp + 1) * 130].rearrange("(j p) c -> p j c", p=128)
                for j0 in range(0, M, 16):
                    self.dma(vb[:, j0:j0 + 16, :], vsrc[:, j0:j0 + 16, :])
                for m in range(M):
                    if m == 0:
                        ch = [(j, 5 + j) for j in range(4)]
                    elif m == 1:
                        ch = [(j, 9 + j) for j in range(4)]
                    elif m == M - 2:
                        ch = [(M - 4 + j, 13 + j) for j in range(4)]
                    elif m == M - 1:
                        ch = [(M - 4 + j, 17 + j) for j in range(4)]
                    else:
                        ch = [(m - 2 + j, j) for j in range(5)]
                    par = u % 2
                    u += 1
                    sbk = [par * 3, par * 3 + 1, par * 3 + 2]
                    Sh = [self.bank(sbk[0]), self.bank(sbk[1])]
                    cfg0 = ch[0][1]
                    for h in range(2):
                        hg = hp * 2 + h
                        for ci in range(4):
                            ktile = ch[ci][0]
                            self.mm(sbk[h], Sh[h][:, ci * 128:(ci + 1) * 128],
                                    kt[:, ktile * 128:(ktile + 1) * 128],
                                    qt[h][:, m * 128:(m + 1) * 128])
                    Pl = [P0[par], P1[par]]
                    for h in range(2):
                        hg = hp * 2 + h
                        self.act(Pl[h][:], Sh[h][:], AF.Exp, scale=0.125)
                        self.tt('dve', Pl[h][:], Pl[h][:], biasT[:, hg, cfg0 * 128:(cfg0 + 4) * 128], ALU.mult)
                    if len(ch) == 5:
                        S2 = self.bank(sbk[2])
                        ktile = ch[4][0]
                        for h in range(2):
                            self.mm(sbk[2], S2[:, h * 128:(h + 1) * 128],
                                    kt[:, ktile * 128:(ktile + 1) * 128],
                                    qt[h][:, m * 128:(m + 1) * 128])
                        self.act(P2[par][:], S2[:, 0:256], AF.Exp, scale=0.125)
                        for h in range(2):
                            hg = hp * 2 + h
                            self.tt('dve', P2[par][:, h * 128:(h + 1) * 128], P2[par][:, h * 128:(h + 1) * 128],
                                    biasT[:, hg, 4 * 128:5 * 128], ALU.mult)
                    ob_ = 6 + par
                    O_ = self.bank(ob_)
                    for h in range(2):
                        for ci in range(len(ch)):
                            ktile = ch[ci][0]
                            lt = Pl[h][:, ci * 128:(ci + 1) * 128] if ci < 4 else P2[par][:, h * 128:(h + 1) * 128]
                            self.mm(ob_, O_[:, h * 65:(h + 1) * 65], lt, vb[:, ktile, h * 65:(h + 1) * 65])
                    Ov = O_[:, 0:130].rearrange("p (h c) -> p h c", c=65)
                    rd = rden[par]
                    self.I('dve', lambda e, o=rd[:], i=Ov[:, :, 64]: e.reciprocal(o, i), [rd[:]], [O_[:]])
                    os_ = obst[(m // 4) % 2]
                    ov = os_[:, m % 4, :].rearrange("p (h c) -> p h c", c=64)
                    self.tt('dve', ov, Ov[:, :, 0:64], rd[:].unsqueeze(2).to_broadcast([128, 2, 64]), ALU.mult)
                    if m % 4 == 3:
                        m0 = m - 3
                        dst = self.OB[t0 + m0 * 128:t0 + (m0 + 4) * 128, hp * 128:(hp + 1) * 128]
                        self.dma(dst.rearrange("(j p) c -> p j c", p=128), os_[:], q=self.stq)

    def phase3a(self):
        self.S.barrier()
        self.reset_sb()
        self.common_consts()
        gcol = self.sb("gcol", [128, 8], F32)
        self.dma(gcol[:], self.gmix)
        Wout = self.sb("Wout", [128, 8, D], BF16)
        Wxq = self.sb("Wxq", [128, 8, D], BF16)
        Wxo = self.sb("Wxo", [128, 8, D], BF16)
        gb = [self.sb(f"gb{i}", [128, D], F32) for i in range(4)]
        KmT_l = [self.sb(f"KmT{i}", [128, 8, NMEM], BF16) for i in range(len(self.seqs))]
        Vm_l = [self.sb(f"Vm{i}", [128, 2, 4 * 257], BF16) for i in range(len(self.seqs))]
        mark = self.sb_off
        self.wstage = [self.sb("wst0", [128, 2048], F32), self.sb("wst1", [128, 2048], F32)]
        Wxkv = self.sb("Wxkv", [128, 8, 2 * D], BF16)
        memT = self.sb("memT", [128, 8, NMEM], BF16)
        mf = self.sb("mf", [128, D], F32)
        mb = self.sb("mb", [128, D], BF16)
        self.sb_off = mark
        NSL = int(os.environ.get('NSLOT', '2'))
        od = [self.sb(f"od{i}", [128, 3, 520], F32) for i in range(NSL)]
        obt = [self.sb(f"obt{i}", [128, 512], F32) for i in range(NSL)]
        x0 = [self.sb(f"x0_{i}", [128, D], F32) for i in range(NSL)]
        oa_l = [self.sb(f"oa{i}", [128, 512], F32) for i in range(NSL)]
        osum_l = [self.sb(f"osum{i}", [128, 520], F32) for i in range(NSL)]
        rd8_l = [self.sb(f"rd8{i}", [128, 8], F32) for i in range(NSL)]
        ss_l = [self.sb(f"ss{i}", [128, 4], F32) for i in range(NSL)]
        junk = self.sb("junk", [128, 512], BF16)
        yb_l = [self.sb(f"yb{i}", [128, D], BF16) for i in range(NSL)]
        yT_l = [self.sb(f"yT{i}", [128, 8, 128], BF16) for i in range(NSL)]
        rr_l = [self.sb(f"rr{i}", [128, D], F32) for i in range(NSL)]
        x1_l = [self.sb(f"x1_{i}", [128, D], F32) for i in range(8)]
        x1b_l = [self.sb(f"x1b{i}", [128, D], BF16) for i in range(NSL)]
        NX = int(os.environ.get("NX", "1"))
        x1T_l = [self.sb(f"x1T{i}", [128, 8, 512], BF16) for i in range(NX)]
        qxT_l = [self.sb(f"qxT{i}", [128, 8, 512], BF16) for i in range(NX)]
        Px_l = [self.sb(f"Px{i}", [128, 512], BF16) for i in range(8 * NX)]
        ox_l = [self.sb(f"ox{i}", [128, D], BF16) for i in range(NSL)]
        oT_l = [self.sb(f"oT{i}", [128, 8, 128], BF16) for i in range(NSL)]
        rdx_l = [self.sb(f"rdx{i}", [128, 1], F32) for i in range(4)]
        self.load_w(Wout, self.w_out, D, rowscale=gcol)
        self.load_w(Wxq, self.w_xq, D)
        self.load_w(Wxo, self.w_xo, D)
        for i in range(4):
            self.bcast_row(gb[i][:], self.lnv[2 + i:3 + i, :])
        self.S.barrier()
        self.load_w(Wxkv, self.w_xkv, 2 * D)
        for si, t0, T in self.seq_iter():
            KmT, Vm = KmT_l[si], Vm_l[si]
            for mc in range(2):
                self.dma(mf[:], self.mem[si * NMEM + mc * 128:si * NMEM + (mc + 1) * 128, :])
                self.copy('act', mb[:], mf[:])
                self.transpose8(mb[:], memT[:, :, mc * 128:(mc + 1) * 128], 0)
            for f in range(8):
                bb = 2 + f % 2
                p1 = self.bank(bb)
                for c in range(8):
                    self.mm(bb, p1[:, 0:NMEM], Wxkv[:, c, f * 128:(f + 1) * 128], memT[:, c, :])
                self.copy('act', KmT[:, f, :], p1[:, 0:NMEM])
            self.memset('pool', Vm[:], 1.0)
            for mc in range(2):
                for n in range(2):
                    bb = 4 + n
                    p1 = self.bank(bb)
                    for c in range(8):
                        self.mm(bb, p1[:], memT[:, c, mc * 128:(mc + 1) * 128],
                                Wxkv[:, c, D + n * 512:D + (n + 1) * 512])
                    dstv = Vm[:, mc, n * 514:(n + 1) * 514].rearrange("p (h c) -> p h c", c=257)[:, :, 0:256]
                    self.copy('dve', dstv, p1[:].rearrange("p (h c) -> p h c", c=256))
        self.S.barrier()
        for si, t0, T in self.seq_iter():
            KmT, Vm = KmT_l[si], Vm_l[si]
            ntile = T // 128

            def load(k):
                g0 = t0 + k * 128
                sl = k % NSL
                self.dma(od[sl][:], self.OD[:, g0:g0 + 128, :].rearrange("d p c -> p d c"))
                self.dma(obt[sl][:], self.OB[g0:g0 + 128, :])
                self.dma(x0[sl][:], self.X0[g0:g0 + 128, :])

            load(0)
            for s0 in range(0, T, 512):
                sx = (s0 // 512) % NX
                x1T, qxT, Px = x1T_l[sx], qxT_l[sx], Px_l[sx * 8:sx * 8 + 8]
                for ti in range(4):
                    k = s0 // 128 + ti
                    g0 = t0 + k * 128
                    sl = k % NSL
                    if k + 1 < ntile:
                        load(k + 1)
                    od_, ob_, x0_ = od[sl], obt[sl], x0[sl]
                    oa, osum, rd8, ss, yb, yT, rr_, x1b = oa_l[sl], osum_l[sl], rd8_l[sl], ss_l[sl], yb_l[sl], yT_l[sl], rr_l[sl], x1b_l[sl]
                    x1 = x1_l[((s0 // 512) % 2) * 4:((s0 // 512) % 2) * 4 + 4]
                    self.tt('pool', osum[:], od_[:, 0, :], od_[:, 1, :], ALU.add)
                    self.tt('pool', osum[:], osum[:], od_[:, 2, :], ALU.add)
                    ov = osum[:].rearrange("p (h c) -> p h c", c=65)
                    self.I('dve', lambda e, o=rd8[:], i=ov[:, :, 64]: e.reciprocal(o, i), [rd8[:]], [osum[:]])
                    self.tt('dve', oa[:].rearrange("p (h c) -> p h c", c=64), ov[:, :, 0:64],
                            rd8[:].unsqueeze(2).to_broadcast([128, 8, 64]), ALU.mult)
                    self.act(junk[:], oa[:], AF.Square, accum_out=ss[:, 0:1])
                    self.act(junk[:], ob_[:], AF.Square, accum_out=ss[:, 1:2])
                    self.ts('pool', ss[:, 2:4], ss[:, 0:2], 1.0 / 512, EPS, ALU.mult, ALU.add)
                    self.tt('pool', ss[:, 2:4], ss[:, 2:4], self.mhalf[:, 0:1].to_broadcast([128, 2]), ALU.pow)
                    self.act(yb[:, 0:512], oa[:], AF.Identity, scale=ss[:, 2:3])
                    self.act(yb[:, 512:1024], ob_[:], AF.Identity, scale=ss[:, 3:4])
                    self.transpose8(yb[:], yT[:], 0, eng='act')
                    for n in range(2):
                        bb = 2 + n
                        p1 = self.bank(bb)
                        for c in range(8):
                            self.mm(bb, p1[:], yT[:, c, :], Wout[:, c, n * 512:(n + 1) * 512])
                        self.stt(rr_[:, n * 512:(n + 1) * 512], x0_[:, n * 512:(n + 1) * 512], ALPHA, p1[:],
                                 ALU.mult, ALU.add)
                    x1_ = x1[ti]
                    self.layer_norm(rr_[:], gb[0][:], gb[1][:], x1_[:], x1b[:])
                    self.transpose8(x1b[:], x1T[:, :, ti * 128:(ti + 1) * 128], 1, eng='act')
                for f in range(8):
                    bb = 4 + f % 2
                    p1 = self.bank(bb)
                    for c in range(8):
                        self.mm(bb, p1[:], Wxq[:, c, f * 128:(f + 1) * 128], x1T[:, c, :])
                    self.copy(self.rot(('act', 'dve')), qxT[:, f, :], p1[:])
                for hx in range(4):
                    for mc in range(2):
                        bb = 6 + mc
                        p1 = self.bank(bb)
                        for kc in range(2):
                            self.mm(bb, p1[:], KmT[:, hx * 2 + kc, mc * 128:(mc + 1) * 128], qxT[:, hx * 2 + kc, :])
                        self.act(Px[hx * 2 + mc][:], p1[:], AF.Exp, scale=1.0 / 16)
                for ti in range(4):
                    g0 = t0 + s0 + ti * 128
                    ox, oT, rr_ = ox_l[ti % NSL], oT_l[ti % NSL], rr_l[ti % NSL]
                    for hx in range(4):
                        rdx = rdx_l[hx]
                        bb = 4 + hx % 2
                        p1 = self.bank(bb)
                        for mc in range(2):
                            self.mm(bb, p1[:, 0:257], Px[hx * 2 + mc][:, ti * 128:(ti + 1) * 128],
                                    Vm[:, mc, hx * 257:(hx + 1) * 257])
                        self.I('dve', lambda e, o=rdx[:], i=p1[:, 256:257]: e.reciprocal(o, i), [rdx[:]], [p1[:]])
                        self.act(ox[:, hx * 256:(hx + 1) * 256], p1[:, 0:256], AF.Identity, scale=rdx[:, 0:1])
                    self.transpose8(ox[:], oT[:], 4, eng='act')
                    x1_ = x1[ti]
                    for n in range(2):
                        bb = 6 + n
                        p1 = self.bank(bb)
                        for c in range(8):
                            self.mm(bb, p1[:], oT[:, c, :], Wxo[:, c, n * 512:(n + 1) * 512])
                        self.stt(rr_[:, n * 512:(n + 1) * 512], x1_[:, n * 512:(n + 1) * 512], ALPHA, p1[:],
                                 ALU.mult, ALU.add)
                    x2_ = x1_
                    self.layer_norm(rr_[:], gb[2][:], gb[3][:], x2_[:], None)
                    self.dma(self.X2[g0:g0 + 128, :], x2_[:])

    def phase3b(self):
        self.S.barrier()
        self.reset_sb()
        self.common_consts()
        Wup = self.sb("Wup", [128, 8, 4 * D], BF16)
        Wdn = self.sb("Wdn", [128, 32, D], BF16)
        gb = [self.sb(f"gb{i}", [128, D], F32) for i in range(2)]
        mark = self.sb_off
        self.wstage = [self.sb("wst0", [128, 4096], F32), self.sb("wst1", [128, 4096], F32)]
        self.sb_off = mark
        NS = 256
        x2 = [self.sb(f"x2_{i}", [128, D], F32) for i in range(4)]
        x2b_l = [self.sb(f"x2b{i}", [128, D], BF16) for i in range(2)]
        x2T = [self.sb(f"x2T{i}", [128, 8, NS], BF16) for i in range(2)]
        hT = self.sb("hT", [128, 32, NS], BF16)
        rl = [self.sb(f"rl{i}", [128, NS], F32) for i in range(4)]
        rr_l = [self.sb(f"rr{i}", [128, D], F32) for i in range(2)]
        yo = [self.sb(f"yo{i}", [128, D], F32) for i in range(2)]
        for c in range(8):
            self.dma(Wup[:, c, :], self.WUPb[c * 128:(c + 1) * 128, :])
        for c in range(8):
            self.dma(Wdn[:, c * 4:(c + 1) * 4, :], self.WDNb[c * 512:(c + 1) * 512, :].rearrange("(j p) n -> p j n", p=128))
        for i in range(2):
            self.bcast_row(gb[i][:], self.lnv[6 + i:7 + i, :])
        self.S.barrier()
        items = [(t0 + s0) for si, t0, T in self.seq_iter() for s0 in range(0, T, NS)]

        def load(i):
            for ti in range(NS // 128):
                g0 = items[i] + ti * 128
                self.dma(x2[(i % 2) * 2 + ti][:], self.X2[g0:g0 + 128, :])

        load(0)
        ny = 0
        for i in range(len(items)):
            sl = i % 2
            if i + 1 < len(items):
                load(i + 1)
            for ti in range(NS // 128):
                x2_ = x2[sl * 2 + ti]
                x2b = x2b_l[ti % 2]
                self.copy('act', x2b[:], x2_[:])
                self.transpose8(x2b[:], x2T[sl][:, :, ti * 128:(ti + 1) * 128], ti % 2)
            for f in range(32):
                bb = (2, 3, 6, 7)[f % 4]
                p1 = self.bank(bb)
                for c in range(8):
                    self.mm(bb, p1[:, 0:NS], Wup[:, c, f * 128:(f + 1) * 128], x2T[sl][:, c, :])
                r_ = rl[f % 4]
                self.act(r_[:], p1[:, 0:NS], AF.Relu)
                self.tt('dve', hT[:, f, :], r_[:], r_[:], ALU.mult)
            for ti in range(NS // 128):
                g0 = items[i] + ti * 128
                x2_ = x2[sl * 2 + ti]
                rr_ = rr_l[ti % 2]
                for n in range(2):
                    bb = 4 + n
                    p1 = self.bank(bb)
                    for f in range(32):
                        self.mm(bb, p1[:], hT[:, f, ti * 128:(ti + 1) * 128], Wdn[:, f, n * 512:(n + 1) * 512])
                    self.stt(rr_[:, n * 512:(n + 1) * 512], x2_[:, n * 512:(n + 1) * 512], ALPHA, p1[:],
                             ALU.mult, ALU.add)
                y_ = yo[ny % 2]
                ny += 1
                self.layer_norm(rr_[:], gb[0][:], gb[1][:], y_[:], None)
                self.dma(self.y[g0:g0 + 128, :], y_[:])

    def emit(self):
        nc = self.nc
        S = self.S
        S.prepare()
        sems = {e: nc.alloc_semaphore(f"s_{e}") for e in ENGS}
        dsems = {q: [nc.alloc_semaphore(f"d_{q}_{i}") for i in range(S.NDMA)] for q in ('sp', 'pool', 'act')}
        dsems['pe'] = dsems['dve'] = []
        print("[sched] ops per engine", {e: len(S.stream[e]) for e in ENGS}, "est_ms", round(S.est_total / 1e6, 3),
              "sig", {e: max([o.cnt for o in S.stream[e] if isinstance(o, Op)] or [0]) for e in ENGS}, flush=True)
        with nc.Block() as block:
            @block.tensor
            def _(eng):
                S.emit_engine('pe', eng, sems, dsems)

            @block.scalar
            def _(eng):
                S.emit_engine('act', eng, sems, dsems)

            @block.vector
            def _(eng):
                S.emit_engine('dve', eng, sems, dsems)

            @block.gpsimd
            def _(eng):
                S.emit_engine('pool', eng, sems, dsems)

            @block.sync
            def _(eng):
                S.emit_engine('sp', eng, sems, dsems)


def host_inputs(inputs, seqs_cfg):
    ridx, cidx, mask = _bias_tables()
    rpb = np.asarray(inputs['rpb'], np.float32)[0]
    biasG = np.ascontiguousarray(np.stack([rpb[h][ridx, cidx] for h in range(8)], 1))
    Tmax = max(max(c) for c in seqs_cfg)
    cosT, sinT, ident, band, perm = _consts(Tmax)
    lnv = np.ascontiguousarray(np.stack([
        np.asarray(inputs['ln_in_g']), np.asarray(inputs['ln_in_b']),
        np.asarray(inputs['ln1_g'])[0], np.asarray(inputs['ln1_b'])[0],
        np.asarray(inputs['ln2_g'])[0], np.asarray(inputs['ln2_b'])[0],
        np.asarray(inputs['ln3_g'])[0], np.asarray(inputs['ln3_b'])[0]], 0).astype(np.float32))
    gmix = np.ascontiguousarray(np.concatenate([np.asarray(inputs['g_mix_a'])[0], np.asarray(inputs['g_mix_b'])[0]]).reshape(8, 128).T)
    shared = dict(
        w_in=np.ascontiguousarray(np.asarray(inputs['w_in'])[0]), w_out=np.ascontiguousarray(np.asarray(inputs['w_out'])[0]),
        w_xq=np.ascontiguousarray(np.asarray(inputs['w_xq'])[0]), w_xkv=np.ascontiguousarray(np.asarray(inputs['w_xkv'])[0]),
        w_xo=np.ascontiguousarray(np.asarray(inputs['w_xo'])[0]), w_up=np.ascontiguousarray(np.asarray(inputs['w_up'])[0]),
        w_down=np.ascontiguousarray(np.asarray(inputs['w_down'])[0]),
        lnv=lnv, gmix=gmix, biasG=biasG, maskC=mask, cosT=cosT, sinT=sinT, ident=ident, band=band, perm=perm)
    return shared


def kernel(**inputs):
    n = 8
    xp = np.asarray(inputs['x_prompt'])
    xs = np.asarray(inputs['x_sample'])
    mp = np.asarray(inputs['mem_prompt'])
    ms = np.asarray(inputs['mem_sample'])
    Tp, Ts = xp.shape[1], xs.shape[1]
    seqs = [Tp, Tp, Ts]
    shared = host_inputs(inputs, [seqs])
    nc = Builder(seqs).build()
    in_maps = []
    for c in range(n):
        m = dict(shared)
        m['x'] = np.ascontiguousarray(np.concatenate([xp[2 * c], xp[2 * c + 1], xs[c]], 0))
        m['mem'] = np.ascontiguousarray(np.concatenate([mp[2 * c], mp[2 * c + 1], ms[c]], 0))
        in_maps.append(m)
    res = run_bass_kernel_spmd(nc, in_maps, core_ids=list(range(n)))
    yp = np.empty(xp.shape, np.float32)
    ys = np.empty(xs.shape, np.float32)
    for c in range(n):
        y = res.results[c]['y']
        yp[2 * c] = y[0:Tp]
        yp[2 * c + 1] = y[Tp:2 * Tp]
        ys[c] = y[2 * Tp:2 * Tp + Ts]
    return (yp, ys)
```

```python
import os
import numpy as np
import concourse.bass as bass
import concourse.mybir as mybir
from concourse.bass_utils import run_bass_kernel_spmd

F32 = mybir.dt.float32
BF16 = mybir.dt.bfloat16
AF = mybir.ActivationFunctionType
ALU = mybir.AluOpType
D = 1024
ALPHA = float(2.0 ** 0.25)
EPS = 1e-5
NEGM = -240000.0
PAD = 1024
NMEM = 256
SB_BASE = 16640
ENGS = ('pe', 'act', 'dve', 'pool', 'sp')


class Op:
    __slots__ = ('eng', 'fn', 'deps', 'dma', 'cost', 'lat', 'seg', 'idx', 'pos', 'succ', 'ndep', 'ready', 'done', 'dk', 'sig', 'cnt')

    def __init__(self, eng, fn, deps, dma, cost, lat, seg, idx):
        self.eng, self.fn, self.deps, self.dma, self.cost, self.lat, self.seg, self.idx = eng, fn, deps, dma, cost, lat, seg, idx
        self.sig = False


class Sched:
    NDMA = 12

    def __init__(self):
        self.all = []
        self.lastw = {}
        self.readers = {}
        self.seg = 0
        self.reorder = True

    def add(self, eng, fn, reads=(), writes=(), dma=False, cost=300.0, lat=0.0):
        deps = set()
        for r in reads:
            w = self.lastw.get(r)
            if w is not None:
                deps.add(w)
            if r.startswith('ps'):
                for x in self.readers.get(r, ()):
                    if x.eng != eng:
                        deps.add(x)
        for w_ in writes:
            w = self.lastw.get(w_)
            if w is not None:
                deps.add(w)
            deps.update(self.readers.get(w_, ()))
        op = Op(eng, fn, deps, dma, cost, lat, self.seg, len(self.all))
        self.all.append(op)
        for r in reads:
            self.readers.setdefault(r, []).append(op)
        for w_ in writes:
            self.lastw[w_] = op
            self.readers[w_] = []
        return op

    def barrier(self):
        self.seg += 1
        self.lastw.clear()
        self.readers.clear()

    def _schedule_segment(self, ops):
        import heapq
        out = {e: [] for e in ENGS}
        if not self.reorder:
            for o in ops:
                out[o.eng].append(o)
            return out
        for o in ops:
            o.succ = []
            o.ndep = 0
        for o in ops:
            for d in o.deps:
                d.succ.append(o)
                o.ndep += 1
        mode = os.environ.get('PRIO', 'bl')
        for o in reversed(ops):
            b = 0.0
            for s in o.succ:
                if s.ready > b:
                    b = s.ready
            o.ready = b + o.cost + o.lat
        for o in ops:
            o.pos = (-o.ready, o.idx) if mode == 'bl' else (o.idx, o.idx)
        free = {e: 0.0 for e in ENGS}
        ht = {e: [] for e in ENGS}
        hi = {e: [] for e in ENGS}
        for o in ops:
            if o.ndep == 0:
                o.ready = 0.0
                heapq.heappush(ht[o.eng], (0.0, o.pos, o))
        left = len(ops)
        while left:
            best = None
            for e in ENGS:
                t_, i_ = ht[e], hi[e]
                while t_ and t_[0][0] <= free[e]:
                    r, ix, o = heapq.heappop(t_)
                    heapq.heappush(i_, (ix, id(o), o))
                if i_:
                    st = free[e]
                elif t_:
                    st = t_[0][0]
                else:
                    continue
                if best is None or st < best[0]:
                    best = (st, e)
            st, e = best
            if hi[e]:
                ix, _, o = heapq.heappop(hi[e])
            else:
                r, ix, o = heapq.heappop(ht[e])
            free[e] = st + o.cost
            o.done = st + o.cost + o.lat
            out[e].append(o)
            left -= 1
            for s in o.succ:
                s.ndep -= 1
                if s.ndep == 0:
                    s.ready = max(d.done for d in s.deps) + float(os.environ.get("HOPLAT", "0"))
                    heapq.heappush(ht[s.eng], (s.ready, s.pos, s))
        self.est = max(self.est, max(free.values())) if hasattr(self, 'est') else max(free.values())
        return out

    def prepare(self):
        nseg = self.seg + 1
        segs = [[] for _ in range(nseg)]
        for o in self.all:
            segs[o.seg].append(o)
        self.stream = {e: [] for e in ENGS}
        self.est_total = 0.0
        for si, ops in enumerate(segs):
            if si > 0:
                prev = self.prev_sched
                deps = set()
                for e in ENGS:
                    if prev[e]:
                        deps.add(prev[e][-1])
                    for o in prev[e]:
                        if o.dma:
                            deps.add(o)
                for e in ENGS:
                    self.stream[e].append(('barrier', deps))
            if hasattr(self, 'est'):
                del self.est
            sched = self._schedule_segment(ops)
            self.est_total += getattr(self, 'est', 0.0)
            if os.environ.get('SCHED_V'):
                print(f"[seg {si}] n={len(ops)} est_us={getattr(self, 'est', 0.0) / 1000:.0f} busy_us=",
                      {e: round(sum(o.cost for o in sched[e]) / 1000) for e in ENGS}, flush=True)
            self.prev_sched = sched
            for e in ENGS:
                self.stream[e].extend(sched[e])
        self.ndma_q = {e: 0 for e in ENGS}
        for e in ENGS:
            for p, o in enumerate(self.stream[e]):
                if isinstance(o, Op):
                    o.pos = p
                    if o.dma:
                        o.dk = self.ndma_q[e]
                        self.ndma_q[e] += 1
        for e in ENGS:
            for p, o in enumerate(self.stream[e]):
                for d in self._eff_deps(e, p, o):
                    if not d.dma:
                        d.sig = True
        for e in ENGS:
            c = 0
            for o in self.stream[e]:
                if isinstance(o, Op):
                    if o.sig:
                        c += 1
                    o.cnt = c

    def _eff_deps(self, e, p, o):
        deps = o.deps if isinstance(o, Op) else o[1]
        last = {}
        res = []
        for d in deps:
            if not self._needs_wait(e, p, d):
                continue
            if d.dma:
                res.append(d)
            else:
                cur = last.get(d.eng)
                if cur is None or d.pos > cur.pos:
                    last[d.eng] = d
        res.extend(last.values())
        return res

    def _needs_wait(self, e, p, d):
        if d.dma:
            return True
        if d.eng != e:
            return True
        if e == 'pe':
            return False
        return True

    def emit_engine(self, e, eng, sems, dsems):
        waited = {}

        def wait(key, sem, val):
            if waited.get(key, 0) >= val:
                return
            waited[key] = val
            eng.wait_ge(sem, val)

        for p, o in enumerate(self.stream[e]):
            for d in sorted(self._eff_deps(e, p, o), key=lambda x: x.idx):
                if d.dma:
                    s = d.dk % self.NDMA
                    wait(('d', d.eng, s), dsems[d.eng][s], 16 * (d.dk // self.NDMA + 1))
                else:
                    wait(('e', d.eng), sems[d.eng], d.cnt)
            if not isinstance(o, Op):
                continue
            if o.dma:
                s = o.dk % self.NDMA
                if o.dk >= self.NDMA:
                    wait(('d', e, s), dsems[e][s], 16 * (o.dk // self.NDMA))
                o.fn(eng).then_inc(dsems[e][s], 16)
            else:
                ins = o.fn(eng)
                if o.sig:
                    ins.then_inc(sems[e], 1)
        if e == 'sp':
            for q in ENGS:
                n = self.ndma_q[q]
                for s in range(min(n, self.NDMA)):
                    cntq = (n - s + self.NDMA - 1) // self.NDMA
                    eng.wait_ge(dsems[q][s], 16 * cntq)


def _bias_tables():
    R = 64
    M = R // 2
    cfgs = [(10, 8 + j) for j in range(5)]
    for m in (0, 1):
        cfgs += [(m, j) for j in range(4)]
    for m in (M - 2, M - 1):
        cfgs += [(m, M - 4 + j) for j in range(4)]
    ridx = np.zeros((21, 128, 128), np.int64)
    cidx = np.zeros((21, 128, 128), np.int64)
    mask = np.zeros((21, 128, 128), np.float32)
    k = np.arange(128)
    krl, kc = k // 64, k % 64
    qrl, qc = k // 64, k % 64
    cs = np.clip(qc - 8, 0, 48)
    for ci, (m, kt) in enumerate(cfgs):
        kr = (2 * kt + krl)[:, None]
        qr = (2 * m + qrl)[None, :]
        rs = np.clip(qr - 4, 0, R - 8)
        rok = (kr >= rs) & (kr < rs + 8)
        cok = (kc[:, None] >= cs[None, :]) & (kc[:, None] < cs[None, :] + 16)
        ridx[ci] = np.clip(kr - qr + 7, 0, 14)
        cidx[ci] = np.clip(kc[:, None] - qc[None, :], -15, 15) + 15
        mask[ci] = np.where(rok & cok, 0.0, NEGM)
    return ridx, cidx, mask


def _consts(Tmax):
    half = 32
    inv = (10000.0 ** (-np.arange(half, dtype=np.float32) / half)).astype(np.float32)
    ang = np.arange(Tmax, dtype=np.float32)[:, None] * inv[None, :]
    cos = np.cos(ang).astype(np.float32).T
    sin = np.sin(ang).astype(np.float32).T
    cosT = np.concatenate([cos, cos, cos, cos], 0)
    sinT = np.concatenate([-sin, sin, -sin, sin], 0)
    ident = np.eye(128, dtype=np.float32)
    p = np.arange(128)[:, None]
    q = np.arange(128)[None, :]
    mA = np.where(p >= q, 1.0, 0.0).astype(np.float32)
    mB = np.where(p <= q, 1.0, 0.0).astype(np.float32)
    band = np.concatenate([mA, mB, mA, mB], 1)
    m = np.arange(128)
    perm = np.zeros((128, 128), np.float32)
    perm[(m // 64) * 64 + ((m % 64) + 32) % 64, m] = 1.0
    return np.ascontiguousarray(cosT), np.ascontiguousarray(sinT), ident, np.ascontiguousarray(band), perm


class Builder:
    def __init__(self, seqs, debug=False, phases=5):
        self.seqs = list(seqs)
        self.Ttot = sum(seqs)
        self.Tmax = max(seqs)
        self.debug = debug
        self.phases = phases
        self.nc = bass.Bass("TRN2", target_bir_lowering=False)
        self.S = Sched()
        self.uid = 0
        self.sb_off = SB_BASE
        self.sbnames = set()
        self.fresh = [True] * 8
        self.psi = 0
        self.rr = 0
        self.stq = 'sp'

    def sb(self, name, shape, dtype):
        nbytes = int(np.prod(shape[1:])) * (2 if dtype == BF16 else 4)
        nbytes = (nbytes + 31) // 32 * 32
        self.uid += 1
        off = self.sb_off
        if off + nbytes > 229376 and os.environ.get('NOASSERT'):
            off = SB_BASE
        t = self.nc.alloc_sbuf_tensor_at(f"{name}_{self.uid}", list(shape), dtype, offset=off)
        self.sb_off += nbytes
        assert self.sb_off <= 229376 or os.environ.get('NOASSERT'), (name, self.sb_off)
        self.sbnames.add(t.name)
        return t

    def reset_sb(self):
        self.sb_off = SB_BASE

    def din(self, name, shape, dtype=F32):
        return self.nc.dram_tensor(name, list(shape), dtype, kind="ExternalInput").ap()

    def dscr(self, name, shape, dtype):
        kind = "ExternalOutput" if (self.debug and name in self.debug) else "Internal"
        return self.nc.dram_tensor(name, list(shape), dtype, kind=kind).ap()

    def keys(self, aps):
        ks = []
        for a in aps:
            n = a.tensor.name
            if n in self.sbnames:
                ks.append(n)
        return ks

    def I(self, eng, fn, outs, ins, cost=None):
        if cost is None:
            n = 1
            for s_ in outs[0].shape[1:]:
                n *= s_
            cost = {'act': 220 + n / 1.2, 'dve': 120 + n / 0.96, 'pool': 200 + n * 2.1, 'pe': 110.0}[eng]
        return self.S.add(eng, fn, self.keys(ins), self.keys(outs), cost=cost)

    def dma(self, out, in_, q='sp'):
        n = 1
        for s_ in out.shape:
            n *= s_
        nbytes = n * (2 if out.dtype == BF16 else 4)
        return self.S.add(q, lambda e: e.dma_start(out=out, in_=in_), self.keys([in_]), self.keys([out]), dma=True,
                          cost=(80.0 if q == 'sp' else 1200.0), lat=2000.0 + nbytes / 150.0)

    def bank(self, b):
        self.fresh[b] = True
        return self.ps[b]

    def mm(self, b, out, lhsT, rhs):
        st = self.fresh[b]
        self.fresh[b] = False
        n = 1
        for s_ in rhs.shape[1:]:
            n *= s_
        self.I('pe', lambda e: e.matmul(out, lhsT, rhs, start=st, stop=False, skip_group_check=True),
               [out], [lhsT, rhs], cost=max(n / 2.4 + 4.0, 100.0))

    def tr(self, out, in_):
        ident = self.identb[:]
        self.I('pe', lambda e: e.transpose(out, in_, ident), [out], [in_, ident])

    def _n(self, ap):
        n = 1
        for s_ in ap.shape[1:]:
            n *= s_
        return n

    def act(self, out, in_, func, **kw):
        extra = [v for v in kw.values() if hasattr(v, 'tensor')]
        outs = [out] + ([kw['accum_out']] if 'accum_out' in kw else [])
        n = self._n(out)
        cost = (180 + 0.75 * n) if in_.tensor.name.startswith('ps') else (200 + 1.0 * n)
        self.I('act', lambda e: e.activation(out, in_, func, **kw), outs, [in_] + extra, cost=cost)

    def tt(self, eng, out, in0, in1, op):
        cost = None
        if eng == 'dve':
            n = 1
            for s_ in out.shape[1:]:
                n *= s_
            allb = all(a.dtype == BF16 for a in (out, in0, in1))
            cost = 120 + 0.6 * n if allb else 140 + 1.3 * n
        self.I(eng, lambda e: e.tensor_tensor(out, in0, in1, op), [out], [in0, in1], cost=cost)

    def ts(self, eng, out, in0, s1, s2, op0, op1=None):
        extra = [v for v in (s1, s2) if hasattr(v, 'tensor')]
        if op1 is None:
            self.I(eng, lambda e: e.tensor_scalar(out, in0, s1, None, op0), [out], [in0] + extra)
        else:
            self.I(eng, lambda e: e.tensor_scalar(out, in0, s1, s2, op0, op1), [out], [in0] + extra)

    def stt(self, out, in0, scalar, in1, op0, op1):
        extra = [scalar] if hasattr(scalar, 'tensor') else []
        self.I('dve', lambda e: e.scalar_tensor_tensor(out, in0, scalar, in1, op0, op1), [out], [in0, in1] + extra)

    def copy(self, eng, out, in_):
        n = self._n(out)
        if eng == 'act':
            cost = (180 + 0.75 * n) if in_.tensor.name.startswith('ps') else (200 + 0.8 * n)
            self.I('act', lambda e: e.copy(out, in_), [out], [in_], cost=cost)
        else:
            cost = None
            if eng == 'dve':
                cost = (110 + 0.5 * n) if in_.dtype == BF16 else (120 + 0.85 * n)
            self.I(eng, lambda e: e.tensor_copy(out, in_), [out], [in_], cost=cost)

    def memset(self, eng, ap, v):
        self.I(eng, lambda e: e.memset(ap, v), [ap], [])

    def rot(self, engs):
        self.rr += 1
        return engs[self.rr % len(engs)]

    def bcast_row(self, dst, src_row):
        n = dst.shape[-1]
        src = bass.AP(src_row.tensor, src_row.offset, [[0, 128], [1, n]])
        self.dma(dst, src)

    def layer_norm(self, src, g_bc, b_bc, xf, xb):
        st = self.stats_l[self.lnk % 4]
        mv = self.mv_l[self.lnk % 4]
        self.lnk += 1
        self.I('dve', lambda e: e.bn_stats(st[:, 0:6], src[:, 0:512]), [st[:, 0:6]], [src])
        self.I('dve', lambda e: e.bn_stats(st[:, 6:12], src[:, 512:1024]), [st[:, 6:12]], [src])
        self.I('dve', lambda e: e.bn_aggr(mv[:, 0:2], st[:, 0:12]), [mv[:]], [st[:]])
        self.ts('pool', mv[:, 2:3], mv[:, 1:2], EPS, None, ALU.add)
        self.tt('pool', mv[:, 3:4], mv[:, 2:3], self.mhalf[:, 0:1], ALU.pow)
        self.stt(mv[:, 4:5], mv[:, 0:1], -1.0, mv[:, 3:4], ALU.mult, ALU.mult)
        self.act(xf, src, AF.Identity, bias=mv[:, 4:5], scale=mv[:, 3:4])
        self.tt('dve', xf, xf, g_bc, ALU.mult)
        if xb is not None:
            self.tt('dve', xb, xf, b_bc, ALU.add)
            self.tt('pool', xf, xf, b_bc, ALU.add)
        else:
            self.tt('dve', xf, xf, b_bc, ALU.add)

    def transpose8(self, xb, dstT, bnk, eng='dve'):
        pb = self.bank(bnk)[:].bitcast(BF16)
        for c in range(8):
            self.tr(pb[:, c * 128:(c + 1) * 128], xb[:, c * 128:(c + 1) * 128])
        self.copy(eng, dstT, pb.rearrange("p (c t) -> p c t", c=8))

    def load_w(self, dst, src, ncols, rowscale=None, stage_cols=4096):
        KC = dst.shape[1]
        for c in range(KC):
            stg = self.wstage[c % 2]
            self.dma(stg[:, 0:ncols], src[c * 128:(c + 1) * 128, :])
            eng = self.rot(('dve', 'act'))
            if rowscale is not None:
                self.ts('dve', dst[:, c, :], stg[:, 0:ncols], rowscale[:, c:c + 1], None, ALU.mult)
            else:
                self.copy(eng, dst[:, c, :], stg[:, 0:ncols])

    def build(self):
        nc = self.nc
        Ttot, nseq = self.Ttot, len(self.seqs)
        self.x = self.din("x", [Ttot, D])
        self.mem = self.din("mem", [nseq * NMEM, D])
        self.w_in = self.din("w_in", [D, 3072])
        self.w_out = self.din("w_out", [D, D])
        self.w_xq = self.din("w_xq", [D, D])
        self.w_xkv = self.din("w_xkv", [D, 2 * D])
        self.w_xo = self.din("w_xo", [D, D])
        self.w_up = self.din("w_up", [D, 4 * D])
        self.w_down = self.din("w_down", [4 * D, D])
        self.lnv = self.din("lnv", [8, D])
        self.gmix = self.din("gmix", [128, 8])
        self.biasG = self.din("biasG", [21, 8, 128, 128])
        self.maskC = self.din("maskC", [21, 128, 128])
        self.cosT = self.din("cosT", [128, self.Tmax])
        self.sinT = self.din("sinT", [128, self.Tmax])
        self.identf = self.din("ident", [128, 128])
        self.bandf = self.din("band", [128, 512])
        self.permf = self.din("perm", [128, 128])
        self.y = nc.dram_tensor("y", [Ttot, D], F32, kind="ExternalOutput").ap()
        self.QA = self.dscr("QA", [512, Ttot], BF16)
        self.KA = self.dscr("KA", [512, Ttot], BF16)
        self.QB = self.dscr("QB", [512, Ttot], BF16)
        self.KB = self.dscr("KB", [512, Ttot], BF16)
        self.VA = self.dscr("VA", [Ttot, 520], BF16)
        self.VB = self.dscr("VB", [Ttot, 520], BF16)
        self.X0 = self.dscr("X0", [Ttot, D], F32)
        self.OD = self.dscr("OD", [3, Ttot, 520], F32)
        self.OB = self.dscr("OB", [Ttot, 512], F32)
        self.X2 = self.dscr("X2", [Ttot, D], F32)
        self.WUPb = self.dscr("WUPb", [D, 4 * D], BF16)
        self.WDNb = self.dscr("WDNb", [4 * D, D], BF16)
        self.ps = [nc.alloc_psum_tensor(f"ps{i}", [128, 512], F32) for i in range(8)]
        for p in self.ps:
            self.sbnames.add(p.name)
        self.phase1()
        if self.phases >= 2:
            self.phase2a()
        if self.phases >= 3:
            self.phase2b()
        if self.phases >= 4:
            self.phase3a()
        if self.phases >= 5:
            self.phase3b()
        self.S.barrier()
        self.emit()
        return nc

    def common_consts(self):
        self.identb = self.sb("identb", [128, 128], BF16)
        idf = self.sb("identf", [128, 128], F32)
        self.dma(idf[:], self.identf)
        self.copy('dve', self.identb[:], idf[:])
        self.mhalf = self.sb("mhalf", [128, 1], F32)
        self.memset('pool', self.mhalf[:], -0.5)
        self.stats_l = [self.sb(f"stats{i}", [128, 12], F32) for i in range(4)]
        self.mv_l = [self.sb(f"mv{i}", [128, 8], F32) for i in range(4)]
        self.lnk = 0

    def seq_iter(self):
        t0 = 0
        for si, T in enumerate(self.seqs):
            yield si, t0, T
            t0 += T

    def phase1(self):
        self.S.barrier()
        self.reset_sb()
        self.common_consts()
        Win = self.sb("Win", [128, 8, 3072], BF16)
        permb = self.sb("permb", [128, 128], BF16)
        permf_ = self.sb("permf", [128, 128], F32)
        self.dma(permf_[:], self.permf)
        self.copy('dve', permb[:], permf_[:])
        qraw = [self.sb(f"qraw{i}", [128, 512], BF16) for i in range(3)]
        self.wstage = [self.sb("wst0", [128, 3072], F32), self.sb("wst1", [128, 3072], F32)]
        for c in range(8):
            stg = self.wstage[c % 2]
            self.dma(stg[:], self.w_in[c * 128:(c + 1) * 128, :])
            self.copy('dve', Win[:, c, :], stg[:])
        g_bc = self.sb("g0", [128, D], F32)
        b_bc = self.sb("b0", [128, D], F32)
        self.bcast_row(g_bc[:], self.lnv[0:1, :])
        self.bcast_row(b_bc[:], self.lnv[1:2, :])
        xt = [self.sb(f"xt{i}", [128, D], F32) for i in range(4)]
        xb = [self.sb(f"xb{i}", [128, D], BF16) for i in range(4)]
        xT = [self.sb(f"xT{i}", [128, 8, 512], BF16) for i in range(2)]
        cs = [self.sb(f"cos{i}", [128, 512], F32) for i in range(2)]
        sn = [self.sb(f"sin{i}", [128, 512], F32) for i in range(2)]
        t1 = [self.sb(f"t1_{i}", [128, 512], F32) for i in range(3)]
        t2 = [self.sb(f"t2_{i}", [128, 512], F32) for i in range(3)]
        qst = [self.sb(f"qst{i}", [128, 512], BF16) for i in range(4)]
        vst = [self.sb(f"vst{i}", [128, 8, 65], BF16) for i in range(4)]
        for v in vst:
            self.memset('pool', v[:], 1.0)
        it = 0
        nq = 0
        nv = 0
        for si, t0, T in self.seq_iter():
            for s0 in range(0, T, 512):
                sl = it % 2
                it += 1
                self.dma(cs[sl][:], self.cosT[:, s0:s0 + 512])
                self.dma(sn[sl][:], self.sinT[:, s0:s0 + 512])
                for ti in range(4):
                    g0 = t0 + s0 + ti * 128
                    x_ = xt[ti]
                    xb_ = xb[ti]
                    self.dma(x_[:], self.x[g0:g0 + 128, :])
                    self.layer_norm(x_[:], g_bc[:], b_bc[:], x_[:], xb_[:])
                    self.dma(self.X0[g0:g0 + 128, :], x_[:])
                    self.transpose8(xb_[:], xT[sl][:, :, ti * 128:(ti + 1) * 128], ti % 2)
                xTs = xT[sl]
                for f in range(8):
                    b1, b2 = 2 + (f % 2) * 2, 3 + (f % 2) * 2
                    p1 = self.bank(b1)
                    p2 = self.bank(b2)
                    for c in range(8):
                        self.mm(b1, p1[:], Win[:, c, f * 128:(f + 1) * 128], xTs[:, c, :])
                    qr_ = qraw[f % 3]
                    self.copy('act', qr_[:], p1[:])
                    self.mm(b2, p2[:], permb[:], qr_[:])
                    a, b_ = t1[f % 3], t2[f % 3]
                    self.tt('dve', a[:], p1[:], cs[sl][:], ALU.mult)
                    self.tt('dve', b_[:], p2[:], sn[sl][:], ALU.mult)
                    q_ = qst[nq % 4]
                    nq += 1
                    self.tt('pool', q_[:], a[:], b_[:], ALU.add)
                    dst = self.QA if f < 4 else self.KA
                    r0 = (f % 4) * 128
                    self.dma(dst[r0:r0 + 128, t0 + s0:t0 + s0 + 512], q_[:])
                for f in range(8):
                    bb = 6 + (f % 2)
                    p1 = self.bank(bb)
                    col = 1536 + f * 128
                    for c in range(8):
                        self.mm(bb, p1[:], Win[:, c, col:col + 128], xTs[:, c, :])
                    q_ = qst[nq % 4]
                    nq += 1
                    self.copy('act', q_[:], p1[:])
                    dst = self.QB if f < 4 else self.KB
                    r0 = (f % 4) * 128
                    self.dma(dst[r0:r0 + 128, t0 + s0:t0 + s0 + 512], q_[:])
                for ti in range(4):
                    g0 = t0 + s0 + ti * 128
                    for gi, (col, dst) in enumerate(((1024, self.VA), (2560, self.VB))):
                        bb = 2 + (nv % 4)
                        p1 = self.bank(bb)
                        for c in range(8):
                            self.mm(bb, p1[:], xTs[:, c, ti * 128:(ti + 1) * 128], Win[:, c, col:col + 512])
                        v_ = vst[nv % 4]
                        nv += 1
                        self.copy(self.rot(('act', 'dve')), v_[:, :, 0:64], p1[:].rearrange("p (h c) -> p h c", h=8))
                        self.dma(dst[g0:g0 + 128, :], v_[:].rearrange("p h c -> p (h c)"))

    def phase2a(self):
        self.S.barrier()
        self.reset_sb()
        self.common_consts()
        Tm = self.Tmax
        bandf = self.sb("bandf", [128, 512], F32)
        bandb = self.sb("bandb", [128, 512], BF16)
        self.dma(bandf[:], self.bandf)
        self.copy('dve', bandb[:], bandf[:])
        QT = [[self.sb(f"QT{i}_{h}", [128, Tm], BF16) for h in range(2)] for i in range(2)]
        KT = [self.sb(f"KT{i}", [128, Tm + 2 * PAD], BF16) for i in range(2)]
        for qq in QT:
            for q_ in qq:
                self.memset('pool', q_[:], 0.0)
        for k_ in KT:
            self.memset('pool', k_[:], 0.0)
        if os.environ.get('DBG2A') == '001':
            return
        ncmax = Tm // 128 + 16
        Vd = [self.sb(f"Vd{i}", [128, ncmax, 130], BF16) for i in range(2)]
        Pt = [self.sb(f"P{i}", [128, 512], BF16) for i in range(6)]
        pcf = self.sb("pcf", [128, 4096], F32)
        pcb = self.sb("pcb", [128, 4096], BF16)
        pc_jobs = [(self.w_up[c * 128:(c + 1) * 128, :], self.WUPb[c * 128:(c + 1) * 128, :]) for c in range(8)]
        for c in range(8):
            pc_jobs.append((self.w_down[c * 512:(c + 1) * 512, :].rearrange("(j p) n -> p j n", p=128),
                            self.WDNb[c * 512:(c + 1) * 512, :].rearrange("(j p) n -> p j n", p=128)))
        ost = [self.sb(f"ost{i}", [128, 2, 130], F32) for i in range(3)]
        ident = self.identb
        it = 0
        vi = 0
        u = 0
        og = 0
        for si, t0, T in self.seq_iter():
            for hp in range(4):
                qt = QT[it % 2]
                kt = KT[it % 2]
                it += 1
                for h in range(2):
                    self.dma(qt[h][h * 64:(h + 1) * 64, 0:T], self.QA[hp * 128 + h * 64:hp * 128 + (h + 1) * 64, t0:t0 + T])
                self.dma(kt[:, PAD:PAD + T], self.KA[hp * 128:(hp + 1) * 128, t0:t0 + T])
                if T < Tm:
                    self.memset('pool', kt[:, PAD + T:PAD + T + PAD], 0.0)
                for di, d in enumerate((1, 4, 16)):
                    n_it = len(self.seqs) * 12
                    i_it = (si * 4 + hp) * 3 + di
                    n_now = ((i_it + 1) * 16 + n_it - 1) // n_it - (i_it * 16 + n_it - 1) // n_it
                    for _ in range(n_now):
                        if pc_jobs:
                            src_, dst_ = pc_jobs.pop(0)
                            if len(src_.shape) == 3:
                                self.dma(pcf[:].rearrange("p (j n) -> p j n", j=4), src_)
                                self.copy('pool', pcb[:], pcf[:])
                                self.dma(dst_, pcb[:].rearrange("p (j n) -> p j n", j=4), q=self.stq)
                            else:
                                self.dma(pcf[:], src_)
                                self.copy('pool', pcb[:], pcf[:])
                                self.dma(dst_, pcb[:], q=self.stq)
                    if os.environ.get('DBGD') and str(d) not in os.environ['DBGD'].split(','):
                        continue
                    L = T // d
                    nb = L // 128
                    ncj = nb + 1
                    vd = Vd[vi % 2]
                    vi += 1
                    va = self.VA[t0:t0 + T, hp * 130:(hp + 1) * 130]
                    if os.environ.get('DBG2A') == '00':
                        continue
                    self.memset('pool', vd[0:64, 0:d * ncj:ncj, :], 0.0)
                    self.memset('pool', vd[64:128, nb:d * ncj:ncj, :], 0.0)
                    self.dma(vd[64:128, 0:d * ncj:ncj, :],
                             va[0:64 * d, :].rearrange("(i r) c -> i r c", r=d))
                    self.dma(vd[0:64, nb:d * ncj:ncj, :],
                             va[T - 64 * d:T, :].rearrange("(i r) c -> i r c", r=d))
                    if os.environ.get('DBG2A') == '01':
                        continue
                    if d <= 4:
                        for r in range(d):
                            sub = va.rearrange("(i r) c -> r i c", r=d)[r]
                            inner = sub[64:64 + 128 * (nb - 1), :].rearrange("(j p) c -> p j c", p=128)
                            for j0 in range(0, nb - 1, 16):
                                j1 = min(nb - 1, j0 + 16)
                                self.dma(vd[:, r * ncj + 1 + j0:r * ncj + 1 + j1, :], inner[:, j0:j1, :])
                    else:
                        for j in range(1, nb):
                            a0 = d * (64 + 128 * (j - 1))
                            self.dma(vd[:, j:d * ncj:ncj, :],
                                     va[a0:a0 + 128 * d, :].rearrange("(p r) c -> p r c", r=d))
                    odv = self.OD[di, t0:t0 + T, hp * 130:(hp + 1) * 130].rearrange("(i r) c -> r i c", r=d)
                    if os.environ.get('DBG2A') == '0':
                        continue
                    for r in range(d):
                        for b in range(nb):
                            sb_ = 2 + (u % 4)
                            u += 1
                            S_ = self.bank(sb_)
                            for h in range(2):
                                if os.environ.get('DBGH') == '0' and h == 1:
                                    continue
                                for s_, j in enumerate((b, b + 1)):
                                    if os.environ.get('DBGH') == 'n':
                                        continue
                                    c0 = PAD + r + d * (128 * j - 64)
                                    q0 = r + d * 128 * b
                                    self.mm(sb_, S_[:, (2 * h + s_) * 128:(2 * h + s_ + 1) * 128],
                                            kt[:, c0:c0 + 127 * d + 1:d],
                                            qt[h][:, q0:q0 + 127 * d + 1:d])
                            P_ = Pt[u % 6]
                            self.act(P_[:], S_[:], AF.Exp, scale=0.125)
                            self.tt('dve', P_[:], P_[:], bandb[:], ALU.mult)
                            if os.environ.get('DBG2A') == '1':
                                continue
                            ob_ = og % 2
                            if b % 2 == 0:
                                O_ = self.bank(ob_)
                            else:
                                O_ = self.ps[ob_]
                            for h in range(2):
                                for s_, j in enumerate((b, b + 1)):
                                    o0 = (b % 2) * 130 + h * 65
                                    self.mm(ob_, O_[:, o0:o0 + 65],
                                            P_[:, (2 * h + s_) * 128:(2 * h + s_ + 1) * 128],
                                            vd[:, r * ncj + j, h * 65:(h + 1) * 65])
                            if b % 2 == 1 or b == nb - 1:
                                gsz = (b % 2) + 1
                                os_ = ost[og % 3]
                                og += 1
                                self.copy('dve', os_[:, 0:gsz, :].rearrange("p a c -> p (a c)"), O_[:, 0:130 * gsz])
                                dst = odv[r].rearrange("(b q) c -> q b c", q=128)[:, b + 1 - gsz:b + 1, :]
                                self.dma(dst, os_[:, 0:gsz, :], q=self.stq)

    def phase2b(self):
        self.S.barrier()
        self.reset_sb()
        self.common_consts()
        Tm = self.Tmax
        biasT = self.sb("biasT", [128, 8, 21 * 128], BF16)
        mC = self.sb("mC", [128, 21, 128], F32)
        bst = [self.sb("bst0", [128, 21, 128], F32)] * 2
        self.dma(mC[:], self.maskC.rearrange("c k q -> k c q"))
        for h in range(8):
            st = bst[h % 2]
            self.dma(st[:], self.biasG[:, h].rearrange("c k q -> k c q"))
            stf = st[:].rearrange("p c q -> p (c q)")
            self.stt(stf, stf, 8.0, mC[:].rearrange("p c q -> p (c q)"), ALU.mult, ALU.add)
            self.act(biasT[:, h, :], stf, AF.Exp, scale=0.125)
        QT = [[self.sb(f"QT{i}_{h}", [128, Tm], BF16) for h in range(2)] for i in range(2)]
        for qq in QT:
            for q_ in qq:
                self.memset('pool', q_[:], 0.0)
        KT = [self.sb(f"KT{i}", [128, Tm], BF16) for i in range(2)]
        VBt = [self.sb(f"VB{i}", [128, Tm // 128, 130], BF16) for i in range(2)]
        P0 = [self.sb(f"Pa{i}", [128, 512], BF16) for i in range(2)]
        P1 = [self.sb(f"Pb{i}", [128, 512], BF16) for i in range(2)]
        P2 = [self.sb(f"Pc{i}", [128, 256], BF16) for i in range(2)]
        rden = [self.sb(f"rden{i}", [128, 2], F32) for i in range(2)]
        obst = [self.sb(f"obst{i}", [128, 4, 128], F32) for i in range(2)]
        ident = self.identb
        it = 0
        u = 0
        for si, t0, T in self.seq_iter():
            M = T // 128
            for hp in range(4):
                qt, kt, vb = QT[it % 2], KT[it % 2], VBt[it % 2]
                it += 1
                for h in range(2):
                    self.dma(qt[h][h * 64:(h + 1) * 64, 0:T], self.QB[hp * 128 + h * 64:hp * 128 + (h + 1) * 64, t0:t0 + T])
                self.dma(kt[:, 0:T], self.KB[hp * 128:(hp + 1) * 128, t0:t0 + T])
                vsrc = self.VB[t0:t0 + T, hp * 130:(hp + 1) * 130].rearrange("(j p) c -> p j c", p=128)
                for j0 in range(0, M, 16):
                    self.dma(vb[:, j0:j0 + 16, :], vsrc[:, j0:j0 + 16, :])
                for m in range(M):
                    if m == 0:
                        ch = [(j, 5 + j) for j in range(4)]
                    elif m == 1:
                        ch = [(j, 9 + j) for j in range(4)]
                    elif m == M - 2:
                        ch = [(M - 4 + j, 13 + j) for j in range(4)]
                    elif m == M - 1:
                        ch = [(M - 4 + j, 17 + j) for j in range(4)]
                    else:
                        ch = [(m - 2 + j, j) for j in range(5)]
                    par = u % 2
                    u += 1
                    sbk = [par * 3, par * 3 + 1, par * 3 + 2]
                    Sh = [self.bank(sbk[0]), self.bank(sbk[1])]
                    cfg0 = ch[0][1]
                    for h in range(2):
                        hg = hp * 2 + h
                        for ci in range(4):
                            ktile = ch[ci][0]
                            self.mm(sbk[h], Sh[h][:, ci * 128:(ci + 1) * 128],
                                    kt[:, ktile * 128:(ktile + 1) * 128],
                                    qt[h][:, m * 128:(m + 1) * 128])
                    Pl = [P0[par], P1[par]]
                    for h in range(2):
                        hg = hp * 2 + h
                        self.act(Pl[h][:], Sh[h][:], AF.Exp, scale=0.125)
                        self.tt('dve', Pl[h][:], Pl[h][:], biasT[:, hg, cfg0 * 128:(cfg0 + 4) * 128], ALU.mult)
                    if len(ch) == 5:
                        S2 = self.bank(sbk[2])
                        ktile = ch[4][0]
                        for h in range(2):
                            self.mm(sbk[2], S2[:, h * 128:(h + 1) * 128],
                                    kt[:, ktile * 128:(ktile + 1) * 128],
                                    qt[h][:, m * 128:(m + 1) * 128])
                        self.act(P2[par][:], S2[:, 0:256], AF.Exp, scale=0.125)
                        for h in range(2):
                            hg = hp * 2 + h
                            self.tt('dve', P2[par][:, h * 128:(h + 1) * 128], P2[par][:, h * 128:(h + 1) * 128],
                                    biasT[:, hg, 4 * 128:5 * 128], ALU.mult)
                    ob_ = 6 + par
                    O_ = self.bank(ob_)
                    for h in range(2):
                        for ci in range(len(ch)):
                            ktile = ch[ci][0]
                            lt = Pl[h][:, ci * 128:(ci + 1) * 128] if ci < 4 else P2[par][:, h * 128:(h + 1) * 128]
                            self.mm(ob_, O_[:, h * 65:(h + 1) * 65], lt, vb[:, ktile, h * 65:(h + 1) * 65])
                    Ov = O_[:, 0:130].rearrange("p (h c) -> p h c", c=65)
                    rd = rden[par]
                    self.I('dve', lambda e, o=rd[:], i=Ov[:, :, 64]: e.reciprocal(o, i), [rd[:]], [O_[:]])
                    os_ = obst[(m // 4) % 2]
                    ov = os_[:, m % 4, :].rearrange("p (h c) -> p h c", c=64)
                    self.tt('dve', ov, Ov[:, :, 0:64], rd[:].unsqueeze(2).to_broadcast([128, 2, 64]), ALU.mult)
                    if m % 4 == 3:
                        m0 = m - 3
                        dst = self.OB[t0 + m0 * 128:t0 + (m0 + 4) * 128, hp * 128:(hp + 1) * 128]
                        self.dma(dst.rearrange("(j p) c -> p j c", p=128), os_[:], q=self.stq)

    def phase3a(self):
        self.S.barrier()
        self.reset_sb()
        self.common_consts()
        gcol = self.sb("gcol", [128, 8], F32)
        self.dma(gcol[:], self.gmix)
        Wout = self.sb("Wout", [128, 8, D], BF16)
        Wxq = self.sb("Wxq", [128, 8, D], BF16)
        Wxo = self.sb("Wxo", [128, 8, D], BF16)
        gb = [self.sb(f"gb{i}", [128, D], F32) for i in range(4)]
        KmT_l = [self.sb(f"KmT{i}", [128, 8, NMEM], BF16) for i in range(len(self.seqs))]
        Vm_l = [self.sb(f"Vm{i}", [128, 2, 4 * 257], BF16) for i in range(len(self.seqs))]
        mark = self.sb_off
        self.wstage = [self.sb("wst0", [128, 2048], F32), self.sb("wst1", [128, 2048], F32)]
        Wxkv = self.sb("Wxkv", [128, 8, 2 * D], BF16)
        memT = self.sb("memT", [128, 8, NMEM], BF16)
        mf = self.sb("mf", [128, D], F32)
        mb = self.sb("mb", [128, D], BF16)
        self.sb_off = mark
        NSL = int(os.environ.get('NSLOT', '2'))
        od = [self.sb(f"od{i}", [128, 3, 520], F32) for i in range(NSL)]
        obt = [self.sb(f"obt{i}", [128, 512], F32) for i in range(NSL)]
        x0 = [self.sb(f"x0_{i}", [128, D], F32) for i in range(NSL)]
        oa_l = [self.sb(f"oa{i}", [128, 512], F32) for i in range(NSL)]
        osum_l = [self.sb(f"osum{i}", [128, 520], F32) for i in range(NSL)]
        rd8_l = [self.sb(f"rd8{i}", [128, 8], F32) for i in range(NSL)]
        ss_l = [self.sb(f"ss{i}", [128, 4], F32) for i in range(NSL)]
        junk = self.sb("junk", [128, 512], BF16)
        yb_l = [self.sb(f"yb{i}", [128, D], BF16) for i in range(NSL)]
        yT_l = [self.sb(f"yT{i}", [128, 8, 128], BF16) for i in range(NSL)]
        rr_l = [self.sb(f"rr{i}", [128, D], F32) for i in range(NSL)]
        x1_l = [self.sb(f"x1_{i}", [128, D], F32) for i in range(8)]
        x1b_l = [self.sb(f"x1b{i}", [128, D], BF16) for i in range(NSL)]
        NX = int(os.environ.get("NX", "1"))
        x1T_l = [self.sb(f"x1T{i}", [128, 8, 512], BF16) for i in range(NX)]
        qxT_l = [self.sb(f"qxT{i}", [128, 8, 512], BF16) for i in range(NX)]
        Px_l = [self.sb(f"Px{i}", [128, 512], BF16) for i in range(8 * NX)]
        ox_l = [self.sb(f"ox{i}", [128, D], BF16) for i in range(NSL)]
        oT_l = [self.sb(f"oT{i}", [128, 8, 128], BF16) for i in range(NSL)]
        rdx_l = [self.sb(f"rdx{i}", [128, 1], F32) for i in range(4)]
        self.load_w(Wout, self.w_out, D, rowscale=gcol)
        self.load_w(Wxq, self.w_xq, D)
        self.load_w(Wxo, self.w_xo, D)
        for i in range(4):
            self.bcast_row(gb[i][:], self.lnv[2 + i:3 + i, :])
        self.S.barrier()
        self.load_w(Wxkv, self.w_xkv, 2 * D)
        for si, t0, T in self.seq_iter():
            KmT, Vm = KmT_l[si], Vm_l[si]
            for mc in range(2):
                self.dma(mf[:], self.mem[si * NMEM + mc * 128:si * NMEM + (mc + 1) * 128, :])
                self.copy('act', mb[:], mf[:])
                self.transpose8(mb[:], memT[:, :, mc * 128:(mc + 1) * 128], 0)
            for f in range(8):
                bb = 2 + f % 2
                p1 = self.bank(bb)
                for c in range(8):
                    self.mm(bb, p1[:, 0:NMEM], Wxkv[:, c, f * 128:(f + 1) * 128], memT[:, c, :])
                self.copy('act', KmT[:, f, :], p1[:, 0:NMEM])
            self.memset('pool', Vm[:], 1.0)
            for mc in range(2):
                for n in range(2):
                    bb = 4 + n
                    p1 = self.bank(bb)
                    for c in range(8):
                        self.mm(bb, p1[:], memT[:, c, mc * 128:(mc + 1) * 128],
                                Wxkv[:, c, D + n * 512:D + (n + 1) * 512])
                    dstv = Vm[:, mc, n * 514:(n + 1) * 514].rearrange("p (h c) -> p h c", c=257)[:, :, 0:256]
                    self.copy('dve', dstv, p1[:].rearrange("p (h c) -> p h c", c=256))
        self.S.barrier()
        for si, t0, T in self.seq_iter():
            KmT, Vm = KmT_l[si], Vm_l[si]
            ntile = T // 128

            def load(k):
                g0 = t0 + k * 128
                sl = k % NSL
                self.dma(od[sl][:], self.OD[:, g0:g0 + 128, :].rearrange("d p c -> p d c"))
                self.dma(obt[sl][:], self.OB[g0:g0 + 128, :])
                self.dma(x0[sl][:], self.X0[g0:g0 + 128, :])

            load(0)
            for s0 in range(0, T, 512):
                sx = (s0 // 512) % NX
                x1T, qxT, Px = x1T_l[sx], qxT_l[sx], Px_l[sx * 8:sx * 8 + 8]
                for ti in range(4):
                    k = s0 // 128 + ti
                    g0 = t0 + k * 128
                    sl = k % NSL
                    if k + 1 < ntile:
                        load(k + 1)
                    od_, ob_, x0_ = od[sl], obt[sl], x0[sl]
                    oa, osum, rd8, ss, yb, yT, rr_, x1b = oa_l[sl], osum_l[sl], rd8_l[sl], ss_l[sl], yb_l[sl], yT_l[sl], rr_l[sl], x1b_l[sl]
                    x1 = x1_l[((s0 // 512) % 2) * 4:((s0 // 512) % 2) * 4 + 4]
                    self.tt('pool', osum[:], od_[:, 0, :], od_[:, 1, :], ALU.add)
                    self.tt('pool', osum[:], osum[:], od_[:, 2, :], ALU.add)
                    ov = osum[:].rearrange("p (h c) -> p h c", c=65)
                    self.I('dve', lambda e, o=rd8[:], i=ov[:, :, 64]: e.reciprocal(o, i), [rd8[:]], [osum[:]])
                    self.tt('dve', oa[:].rearrange("p (h c) -> p h c", c=64), ov[:, :, 0:64],
                            rd8[:].unsqueeze(2).to_broadcast([128, 8, 64]), ALU.mult)
                    self.act(junk[:], oa[:], AF.Square, accum_out=ss[:, 0:1])
                    self.act(junk[:], ob_[:], AF.Square, accum_out=ss[:, 1:2])
                    self.ts('pool', ss[:, 2:4], ss[:, 0:2], 1.0 / 512, EPS, ALU.mult, ALU.add)
                    self.tt('pool', ss[:, 2:4], ss[:, 2:4], self.mhalf[:, 0:1].to_broadcast([128, 2]), ALU.pow)
                    self.act(yb[:, 0:512], oa[:], AF.Identity, scale=ss[:, 2:3])
                    self.act(yb[:, 512:1024], ob_[:], AF.Identity, scale=ss[:, 3:4])
                    self.transpose8(yb[:], yT[:], 0, eng='act')
                    for n in range(2):
                        bb = 2 + n
                        p1 = self.bank(bb)
                        for c in range(8):
                            self.mm(bb, p1[:], yT[:, c, :], Wout[:, c, n * 512:(n + 1) * 512])
                        self.stt(rr_[:, n * 512:(n + 1) * 512], x0_[:, n * 512:(n + 1) * 512], ALPHA, p1[:],
                                 ALU.mult, ALU.add)
                    x1_ = x1[ti]
                    self.layer_norm(rr_[:], gb[0][:], gb[1][:], x1_[:], x1b[:])
                    self.transpose8(x1b[:], x1T[:, :, ti * 128:(ti + 1) * 128], 1, eng='act')
                for f in range(8):
                    bb = 4 + f % 2
                    p1 = self.bank(bb)
                    for c in range(8):
                        self.mm(bb, p1[:], Wxq[:, c, f * 128:(f + 1) * 128], x1T[:, c, :])
                    self.copy(self.rot(('act', 'dve')), qxT[:, f, :], p1[:])
                for hx in range(4):
                    for mc in range(2):
                        bb = 6 + mc
                        p1 = self.bank(bb)
                        for kc in range(2):
                            self.mm(bb, p1[:], KmT[:, hx * 2 + kc, mc * 128:(mc + 1) * 128], qxT[:, hx * 2 + kc, :])
                        self.act(Px[hx * 2 + mc][:], p1[:], AF.Exp, scale=1.0 / 16)
                for ti in range(4):
                    g0 = t0 + s0 + ti * 128
                    ox, oT, rr_ = ox_l[ti % NSL], oT_l[ti % NSL], rr_l[ti % NSL]
                    for hx in range(4):
                        rdx = rdx_l[hx]
                        bb = 4 + hx % 2
                        p1 = self.bank(bb)
                        for mc in range(2):
                            self.mm(bb, p1[:, 0:257], Px[hx * 2 + mc][:, ti * 128:(ti + 1) * 128],
                                    Vm[:, mc, hx * 257:(hx + 1) * 257])
                        self.I('dve', lambda e, o=rdx[:], i=p1[:, 256:257]: e.reciprocal(o, i), [rdx[:]], [p1[:]])
                        self.act(ox[:, hx * 256:(hx + 1) * 256], p1[:, 0:256], AF.Identity, scale=rdx[:, 0:1])
                    self.transpose8(ox[:], oT[:], 4, eng='act')
                    x1_ = x1[ti]
                    for n in range(2):
                        bb = 6 + n
                        p1 = self.bank(bb)
                        for c in range(8):
                            self.mm(bb, p1[:], oT[:, c, :], Wxo[:, c, n * 512:(n + 1) * 512])
                        self.stt(rr_[:, n * 512:(n + 1) * 512], x1_[:, n * 512:(n + 1) * 512], ALPHA, p1[:],
                                 ALU.mult, ALU.add)
                    x2_ = x1_
                    self.layer_norm(rr_[:], gb[2][:], gb[3][:], x2_[:], None)
                    self.dma(self.X2[g0:g0 + 128, :], x2_[:], q='pool')

    def phase3b(self):
        self.S.barrier()
        self.reset_sb()
        self.common_consts()
        Wup = self.sb("Wup", [128, 8, 4 * D], BF16)
        Wdn = self.sb("Wdn", [128, 32, D], BF16)
        gb = [self.sb(f"gb{i}", [128, D], F32) for i in range(2)]
        mark = self.sb_off
        self.wstage = [self.sb("wst0", [128, 4096], F32), self.sb("wst1", [128, 4096], F32)]
        self.sb_off = mark
        NS = 256
        x2 = [self.sb(f"x2_{i}", [128, D], F32) for i in range(4)]
        x2b_l = [self.sb(f"x2b{i}", [128, D], BF16) for i in range(2)]
        x2T = [self.sb(f"x2T{i}", [128, 8, NS], BF16) for i in range(2)]
        hT = self.sb("hT", [128, 32, NS], BF16)
        rl = [self.sb(f"rl{i}", [128, NS], F32) for i in range(4)]
        rr_l = [self.sb(f"rr{i}", [128, D], F32) for i in range(2)]
        yo = [self.sb(f"yo{i}", [128, D], F32) for i in range(2)]
        for c in range(8):
            self.dma(Wup[:, c, :], self.WUPb[c * 128:(c + 1) * 128, :])
        for c in range(8):
            self.dma(Wdn[:, c * 4:(c + 1) * 4, :], self.WDNb[c * 512:(c + 1) * 512, :].rearrange("(j p) n -> p j n", p=128))
        for i in range(2):
            self.bcast_row(gb[i][:], self.lnv[6 + i:7 + i, :])
        self.S.barrier()
        items = [(t0 + s0) for si, t0, T in self.seq_iter() for s0 in range(0, T, NS)]

        def load(i):
            for ti in range(NS // 128):
                g0 = items[i] + ti * 128
                self.dma(x2[(i % 2) * 2 + ti][:], self.X2[g0:g0 + 128, :])

        load(0)
        ny = 0
        for i in range(len(items)):
            sl = i % 2
            if i + 1 < len(items):
                load(i + 1)
            for ti in range(NS // 128):
                x2_ = x2[sl * 2 + ti]
                x2b = x2b_l[ti % 2]
                self.copy('act', x2b[:], x2_[:])
                self.transpose8(x2b[:], x2T[sl][:, :, ti * 128:(ti + 1) * 128], ti % 2)
            for f in range(32):
                bb = (2, 3, 6, 7)[f % 4]
                p1 = self.bank(bb)
                for c in range(8):
                    self.mm(bb, p1[:, 0:NS], Wup[:, c, f * 128:(f + 1) * 128], x2T[sl][:, c, :])
                r_ = rl[f % 4]
                self.act(r_[:], p1[:, 0:NS], AF.Relu)
                self.tt('dve', hT[:, f, :], r_[:], r_[:], ALU.mult)
            for ti in range(NS // 128):
                g0 = items[i] + ti * 128
                x2_ = x2[sl * 2 + ti]
                rr_ = rr_l[ti % 2]
                for n in range(2):
                    bb = 4 + n
                    p1 = self.bank(bb)
                    for f in range(32):
                        self.mm(bb, p1[:], hT[:, f, ti * 128:(ti + 1) * 128], Wdn[:, f, n * 512:(n + 1) * 512])
                    self.stt(rr_[:, n * 512:(n + 1) * 512], x2_[:, n * 512:(n + 1) * 512], ALPHA, p1[:],
                             ALU.mult, ALU.add)
                y_ = yo[ny % 2]
                ny += 1
                self.layer_norm(rr_[:], gb[0][:], gb[1][:], y_[:], None)
                self.dma(self.y[g0:g0 + 128, :], y_[:], q='pool')

    def emit(self):
        nc = self.nc
        S = self.S
        S.prepare()
        sems = {e: nc.alloc_semaphore(f"s_{e}") for e in ENGS}
        dsems = {q: [nc.alloc_semaphore(f"d_{q}_{i}") for i in range(S.NDMA)] for q in ('sp', 'pool', 'act')}
        dsems['pe'] = dsems['dve'] = []
        print("[sched] ops per engine", {e: len(S.stream[e]) for e in ENGS}, "est_ms", round(S.est_total / 1e6, 3),
              "sig", {e: max([o.cnt for o in S.stream[e] if isinstance(o, Op)] or [0]) for e in ENGS}, flush=True)
        with nc.Block() as block:
            @block.tensor
            def _(eng):
                S.emit_engine('pe', eng, sems, dsems)

            @block.scalar
            def _(eng):
                S.emit_engine('act', eng, sems, dsems)

            @block.vector
            def _(eng):
                S.emit_engine('dve', eng, sems, dsems)

            @block.gpsimd
            def _(eng):
                S.emit_engine('pool', eng, sems, dsems)

            @block.sync
            def _(eng):
                S.emit_engine('sp', eng, sems, dsems)


def host_inputs(inputs, seqs_cfg):
    ridx, cidx, mask = _bias_tables()
    rpb = np.asarray(inputs['rpb'], np.float32)[0]
    biasG = np.ascontiguousarray(np.stack([rpb[h][ridx, cidx] for h in range(8)], 1))
    Tmax = max(max(c) for c in seqs_cfg)
    cosT, sinT, ident, band, perm = _consts(Tmax)
    lnv = np.ascontiguousarray(np.stack([
        np.asarray(inputs['ln_in_g']), np.asarray(inputs['ln_in_b']),
        np.asarray(inputs['ln1_g'])[0], np.asarray(inputs['ln1_b'])[0],
        np.asarray(inputs['ln2_g'])[0], np.asarray(inputs['ln2_b'])[0],
        np.asarray(inputs['ln3_g'])[0], np.asarray(inputs['ln3_b'])[0]], 0).astype(np.float32))
    gmix = np.ascontiguousarray(np.concatenate([np.asarray(inputs['g_mix_a'])[0], np.asarray(inputs['g_mix_b'])[0]]).reshape(8, 128).T)
    shared = dict(
        w_in=np.ascontiguousarray(np.asarray(inputs['w_in'])[0]), w_out=np.ascontiguousarray(np.asarray(inputs['w_out'])[0]),
        w_xq=np.ascontiguousarray(np.asarray(inputs['w_xq'])[0]), w_xkv=np.ascontiguousarray(np.asarray(inputs['w_xkv'])[0]),
        w_xo=np.ascontiguousarray(np.asarray(inputs['w_xo'])[0]), w_up=np.ascontiguousarray(np.asarray(inputs['w_up'])[0]),
        w_down=np.ascontiguousarray(np.asarray(inputs['w_down'])[0]),
        lnv=lnv, gmix=gmix, biasG=biasG, maskC=mask, cosT=cosT, sinT=sinT, ident=ident, band=band, perm=perm)
    return shared


def kernel(**inputs):
    n = 8
    xp = np.asarray(inputs['x_prompt'])
    xs = np.asarray(inputs['x_sample'])
    mp = np.asarray(inputs['mem_prompt'])
    ms = np.asarray(inputs['mem_sample'])
    Tp, Ts = xp.shape[1], xs.shape[1]
    seqs = [Tp, Tp, Ts]
    shared = host_inputs(inputs, [seqs])
    nc = Builder(seqs).build()
    in_maps = []
    for c in range(n):
        m = dict(shared)
        m['x'] = np.ascontiguousarray(np.concatenate([xp[2 * c], xp[2 * c + 1], xs[c]], 0))
        m['mem'] = np.ascontiguousarray(np.concatenate([mp[2 * c], mp[2 * c + 1], ms[c]], 0))
        in_maps.append(m)
    res = run_bass_kernel_spmd(nc, in_maps, core_ids=list(range(n)))
    yp = np.empty(xp.shape, np.float32)
    ys = np.empty(xs.shape, np.float32)
    for c in range(n):
        y = res.results[c]['y']
        yp[2 * c] = y[0:Tp]
        yp[2 * c + 1] = y[Tp:2 * Tp]
        ys[c] = y[2 * Tp:2 * Tp + Ts]
    return (yp, ys)
```
